# Optimizing a Trainium2 kernel written in Bass

```python
import functools
import jax, jax.numpy as jnp
from jax import lax
import numpy as np

D_MODEL = 1024
BATCH = 4
SEQ = 4096
DEPTH = 2
DEC_BATCH = 8
DEC_SEQ = 64
PAST_LEN = 4096

CHUNK = 64
N_HEADS = 16
HEAD_DIM = 64
D_ATT = N_HEADS * HEAD_DIM
D_CONV = D_MODEL
CONV_W = 31
N_PAST_CHUNKS = 8
ATT_PAST = N_PAST_CHUNKS * CHUNK
REL_CLIP = 128
D_FF = 2816
FFN_W = 3
PLE_DIM = 256
N_IN = 2 * D_CONV + 3 * D_ATT + 2 * D_MODEL
SPLITS = [D_CONV, 2 * D_CONV, 2 * D_CONV + D_ATT, 2 * D_CONV + 2 * D_ATT, 2 * D_CONV + 3 * D_ATT]
EPS = 1e-6
NEG_INF = -1e30

kernel_name = "streaming_conformer_hybrid_step"


def rms_norm(x, g):
    xf = x.astype(jnp.float32)
    y = xf * lax.rsqrt(jnp.mean(xf * xf, axis=-1, keepdims=True) + EPS)
    return (y * g.astype(jnp.float32)).astype(x.dtype)


def layer_norm(x, g, b):
    xf = x.astype(jnp.float32)
    mu = jnp.mean(xf, axis=-1, keepdims=True)
    var = jnp.mean(jnp.square(xf - mu), axis=-1, keepdims=True)
    y = (xf - mu) * lax.rsqrt(var + EPS)
    return (y * g.astype(jnp.float32) + b.astype(jnp.float32)).astype(x.dtype)


def causal_dwconv(x, hist, w, b):
    xe = jnp.concatenate([hist, x], axis=1)
    width, c = w.shape
    y = lax.conv_general_dilated(xe, w[:, None, :], window_strides=(1,), padding="VALID",
                                 dimension_numbers=("NWC", "WIO", "NWC"), feature_group_count=c)
    return y + b, xe[:, xe.shape[1] - (width - 1):]


def rel_bias(table, dist):
    return table[:, jnp.clip(dist, -REL_CLIP, REL_CLIP) + REL_CLIP].astype(jnp.float32)


def band_attention_prompt(q, k, v, rel_table):
    b, s, h, dh = q.shape
    nc = s // CHUNK
    n_band = N_PAST_CHUNKS + 1
    pad = ((0, 0), (ATT_PAST, 0), (0, 0), (0, 0))
    kc = jnp.pad(k, pad).reshape(b, nc + N_PAST_CHUNKS, CHUNK, h, dh)
    vc = jnp.pad(v, pad).reshape(b, nc + N_PAST_CHUNKS, CHUNK, h, dh)
    band_k = jnp.concatenate([kc[:, o:o + nc] for o in range(n_band)], axis=2)
    band_v = jnp.concatenate([vc[:, o:o + nc] for o in range(n_band)], axis=2)
    qc = q.reshape(b, nc, CHUNK, h, dh)
    sc = jnp.einsum("bnqhd,bnkhd->bnhqk", qc, band_k).astype(jnp.float32) * (dh ** -0.5)
    kj = jnp.arange(n_band * CHUNK)
    sc = sc + rel_bias(rel_table, jnp.arange(CHUNK)[:, None] + ATT_PAST - kj[None, :])[None, None]
    kpos = jnp.arange(nc)[:, None] * CHUNK - ATT_PAST + kj[None, :]
    sc = jnp.where((kpos >= 0)[None, :, None, None, :], sc, NEG_INF)
    pr = jax.nn.softmax(sc, axis=-1).astype(v.dtype)
    o = jnp.einsum("bnhqk,bnkhd->bnqhd", pr, band_v).reshape(b, s, h * dh)
    keep = min(ATT_PAST, s)
    return o, k[:, s - keep:], v[:, s - keep:]


def band_attention_sample(q, k, v, rel_table, cache_k, cache_v):
    b, t, h, dh = q.shape
    l = cache_k.shape[1]
    ks = jnp.concatenate([cache_k, k], axis=1)
    vs = jnp.concatenate([cache_v, v], axis=1)
    sc = jnp.einsum("bqhd,bkhd->bhqk", q, ks).astype(jnp.float32) * (dh ** -0.5)
    sc = sc + rel_bias(rel_table, jnp.arange(t)[:, None] + l - jnp.arange(l + t)[None, :])[None]
    pr = jax.nn.softmax(sc, axis=-1).astype(v.dtype)
    o = jnp.einsum("bhqk,bkhd->bqhd", pr, vs).reshape(b, t, h * dh)
    return o, ks[:, t:], vs[:, t:]


def trunk_layer(x, p, conv_hist, ffn_hist, attend, prm):
    (norm_mix, w_in, conv_dw, conv_dw_b, conv_ln_g, conv_ln_b, w_conv_out, rel_table, w_att_out,
     b_gate, w_out, norm_ffn, w_ffn_up, ffn_dw, ffn_dw_b, w_ffn_down, norm_ple, w_ple_gate, w_ple_proj) = prm
    b, t, _ = x.shape
    h = rms_norm(x, norm_mix)
    glu_a, glu_b, q, k, v, gates = jnp.split(h @ w_in, SPLITS, axis=-1)
    u = glu_a * jax.nn.sigmoid(glu_b)
    c, conv_state = causal_dwconv(u, conv_hist, conv_dw, conv_dw_b)
    c = jax.nn.silu(layer_norm(c, conv_ln_g, conv_ln_b)) @ w_conv_out
    shp = (b, t, N_HEADS, HEAD_DIM)
    a, k_state, v_state = attend(q.reshape(shp), k.reshape(shp), v.reshape(shp), rel_table)
    a = a @ w_att_out
    g = jax.nn.sigmoid(gates + b_gate)
    x = x + (g[..., :D_MODEL] * c + g[..., D_MODEL:] * a) @ w_out
    h = rms_norm(x, norm_ffn)
    up, gate = jnp.split(h @ w_ffn_up, 2, axis=-1)
    up, ffn_state = causal_dwconv(up, ffn_hist, ffn_dw, ffn_dw_b)
    x = x + (jax.nn.gelu(up, approximate=False) * gate) @ w_ffn_down
    h = rms_norm(x, norm_ple)
    x = x + jax.nn.sigmoid(h @ w_ple_gate) * (p @ w_ple_proj)
    return x, k_state, v_state, conv_state, ffn_state


def setup_inputs(seed: int = 0) -> dict:
    key = jax.random.key(seed)
    ks = iter(jax.random.split(key, 40))

    def nrm(shape, scale=1.0):
        return jax.random.normal(next(ks), shape, jnp.float32) * scale

    def gain(shape):
        return 1.0 + nrm(shape, 0.01)

    l_cache = min(ATT_PAST, PAST_LEN)
    return {
        "x_prompt": nrm((BATCH, SEQ, D_MODEL)),
        "x_sample": nrm((DEC_BATCH, DEC_SEQ, D_MODEL)),
        "p_prompt": nrm((DEPTH, BATCH, SEQ, PLE_DIM)),
        "p_sample": nrm((DEPTH, DEC_BATCH, DEC_SEQ, PLE_DIM)),
        "cache_att_k": nrm((DEPTH, DEC_BATCH, l_cache, N_HEADS, HEAD_DIM)),
        "cache_att_v": nrm((DEPTH, DEC_BATCH, l_cache, N_HEADS, HEAD_DIM)),
        "state_conv": nrm((DEPTH, DEC_BATCH, CONV_W - 1, D_CONV), 0.5),
        "state_ffn_conv": nrm((DEPTH, DEC_BATCH, FFN_W - 1, D_FF)),
        "norm_mix": gain((DEPTH, D_MODEL)),
        "w_in": nrm((DEPTH, D_MODEL, N_IN), D_MODEL ** -0.5),
        "conv_dw": nrm((DEPTH, CONV_W, D_CONV), CONV_W ** -0.5),
        "conv_dw_b": nrm((DEPTH, D_CONV), 0.01),
        "conv_ln_g": gain((DEPTH, D_CONV)),
        "conv_ln_b": nrm((DEPTH, D_CONV), 0.01),
        "w_conv_out": nrm((DEPTH, D_CONV, D_MODEL), D_CONV ** -0.5),
        "rel_table": nrm((DEPTH, N_HEADS, 2 * REL_CLIP + 1), 0.1),
        "w_att_out": nrm((DEPTH, D_ATT, D_MODEL), D_ATT ** -0.5),
        "b_gate": nrm((DEPTH, 2 * D_MODEL), 0.01),
        "w_out": nrm((DEPTH, D_MODEL, D_MODEL), D_MODEL ** -0.5),
        "norm_ffn": gain((DEPTH, D_MODEL)),
        "w_ffn_up": nrm((DEPTH, D_MODEL, 2 * D_FF), D_MODEL ** -0.5),
        "ffn_dw": nrm((DEPTH, FFN_W, D_FF), FFN_W ** -0.5),
        "ffn_dw_b": nrm((DEPTH, D_FF), 0.01),
        "w_ffn_down": nrm((DEPTH, D_FF, D_MODEL), D_FF ** -0.5),
        "norm_ple": gain((DEPTH, D_MODEL)),
        "w_ple_gate": nrm((DEPTH, D_MODEL, D_MODEL), D_MODEL ** -0.5),
        "w_ple_proj": nrm((DEPTH, PLE_DIM, D_MODEL), PLE_DIM ** -0.5),
        "norm_final": gain((D_MODEL,)),
    }


def reference(x_prompt, x_sample, p_prompt, p_sample, cache_att_k, cache_att_v, state_conv, state_ffn_conv,
              norm_mix, w_in, conv_dw, conv_dw_b, conv_ln_g, conv_ln_b, w_conv_out, rel_table, w_att_out,
              b_gate, w_out, norm_ffn, w_ffn_up, ffn_dw, ffn_dw_b, w_ffn_down, norm_ple, w_ple_gate,
              w_ple_proj, norm_final):
    xp, xs = x_prompt, x_sample
    bp = xp.shape[0]
    zero_conv = jnp.zeros((bp, CONV_W - 1, D_CONV), xp.dtype)
    zero_ffn = jnp.zeros((bp, FFN_W - 1, D_FF), xp.dtype)
    kp_l, vp_l, cp_l, fp_l, ks_l, vs_l, cs_l, fs_l = [], [], [], [], [], [], [], []
    for i in range(DEPTH):
        prm = (norm_mix[i], w_in[i], conv_dw[i], conv_dw_b[i], conv_ln_g[i], conv_ln_b[i], w_conv_out[i],
               rel_table[i], w_att_out[i], b_gate[i], w_out[i], norm_ffn[i], w_ffn_up[i], ffn_dw[i],
               ffn_dw_b[i], w_ffn_down[i], norm_ple[i], w_ple_gate[i], w_ple_proj[i])
        xp, kp, vp, cp, fp = trunk_layer(xp, p_prompt[i], zero_conv, zero_ffn, band_attention_prompt, prm)
        attend_s = functools.partial(band_attention_sample, cache_k=cache_att_k[i], cache_v=cache_att_v[i])
        xs, ksn, vsn, csn, fsn = trunk_layer(xs, p_sample[i], state_conv[i], state_ffn_conv[i], attend_s, prm)
        kp_l.append(kp); vp_l.append(vp); cp_l.append(cp); fp_l.append(fp)
        ks_l.append(ksn); vs_l.append(vsn); cs_l.append(csn); fs_l.append(fsn)
    y_prompt = rms_norm(xp, norm_final)
    y_sample = rms_norm(xs, norm_final)
    new_k_prompt = jnp.stack(kp_l)
    new_v_prompt = jnp.stack(vp_l)
    new_conv_prompt = jnp.stack(cp_l)
    new_ffn_prompt = jnp.stack(fp_l)
    new_k_sample = jnp.stack(ks_l)
    new_v_sample = jnp.stack(vs_l)
    new_conv_sample = jnp.stack(cs_l)
    new_ffn_sample = jnp.stack(fs_l)
    return (y_prompt, y_sample, new_k_prompt, new_v_prompt, new_conv_prompt, new_ffn_prompt,
            new_k_sample, new_v_sample, new_conv_sample, new_ffn_sample)
```

```python
import contextlib
import os
import types
import numpy as np
import concourse.bass as bass
import concourse.mybir as mybir
from concourse.bass_utils import run_bass_kernel_spmd

F32 = mybir.dt.float32
BF16 = mybir.dt.bfloat16
AF = mybir.ActivationFunctionType
ALU = mybir.AluOpType
PE, ACT, DVE, POOL, SP = "pe", "act", "dve", "pool", "sp"

D = 1024
NB = 8
NH = 16
DFF = 2816
FB = 22
NIN = 7168
NCHK = 41
HALO = 18
NTOK = NCHK * 64 + 64
TILES = [(t * 512, 512) for t in range(5)] + [(2560, 128)]
EPS = 1e-6
MASKV = -30000.0

VOFF = {}
_c = 0
for _n, _w in [("norm_mix", 8), ("conv_dw", 31 * 8), ("conv_dw_b", 8), ("conv_ln_g", 8), ("conv_ln_b", 8),
               ("b_gate", 16), ("norm_ffn", 8), ("ffn_dw", 3 * 22), ("ffn_dw_b", 22), ("norm_ple", 8)]:
    VOFF[_n] = _c
    _c += _w
VPL = _c
NV = 2 * VPL + 8 + 32

WSPEC = [("w_in", 1024, NIN), ("w_conv_out", 1024, 1024), ("w_att_out", 1024, 1024), ("w_out", 1024, 1024),
         ("w_ffn_up", 1024, 2 * DFF), ("w_ffn_down", DFF, 1024), ("w_ple_gate", 1024, 1024),
         ("w_ple_proj", 256, 1024)]


def _freeze(fn, depth=0):
    if not isinstance(fn, types.FunctionType) or fn.__closure__ is None or depth > 4:
        return fn
    cells = []
    for c in fn.__closure__:
        try:
            v = c.cell_contents
        except ValueError:
            cells.append(c)
            continue
        if isinstance(v, types.FunctionType) and v.__closure__ is not None:
            v = _freeze(v, depth + 1)
        cells.append(types.CellType(v))
    g = types.FunctionType(fn.__code__, fn.__globals__, fn.__name__, fn.__defaults__, tuple(cells))
    g.__kwdefaults__ = fn.__kwdefaults__
    return g


class _Stop(Exception):
    pass


KSTOP = os.environ.get("KSTOP", "")


def chk(tag):
    if KSTOP and tag == KSTOP:
        raise _Stop()


class Prog:
    def __init__(self, nc, stack):
        self.nc = nc
        self.stack = stack
        self.engs = [PE, ACT, DVE, POOL, SP]
        self.ops = {e: [] for e in self.engs}
        self.semobj = {}
        for e in self.engs:
            self.semobj["es_" + e] = stack.enter_context(nc.semaphore("es_" + e))
        self.ecount = {e: 0 for e in self.engs}
        self.waited = {e: {} for e in self.engs}
        self.res = {}
        self.dcount = {}
        self.out_tokens = []

    def _deps(self, eng, reads, writes):
        deps = set()
        for r in reads:
            st = self.res.get(r)
            if st and st["w"]:
                deps.add(st["w"])
        for w in writes:
            st = self.res.get(w)
            if st:
                if st["w"]:
                    deps.add(st["w"])
                deps |= st["r"]
        waits = []
        wd = self.waited[eng]
        own = "es_" + eng
        for (sname, val) in sorted(deps):
            if eng == PE and sname == own:
                continue
            if wd.get(sname, 0) >= val:
                continue
            wd[sname] = val
            waits.append((sname, val))
        return waits

    def _commit(self, tok, reads, writes):
        for r in reads:
            st = self.res.setdefault(r, {"w": None, "r": set()})
            st["r"].add(tok)
        for w in writes:
            st = self.res.setdefault(w, {"w": None, "r": set()})
            st["w"] = tok
            st["r"] = set()

    def op(self, eng, fn, reads=(), writes=()):
        waits = self._deps(eng, reads, writes)
        self.ecount[eng] += 1
        tok = ("es_" + eng, self.ecount[eng])
        self.ops[eng].append((_freeze(fn), waits, ("es_" + eng, 1)))
        self._commit(tok, reads, writes)
        return tok

    def dma(self, eng, fn, sem, reads=(), writes=(), n=1, is_output=False):
        key = "ds_" + sem
        if key not in self.semobj:
            self.semobj[key] = self.stack.enter_context(self.nc.semaphore(key))
            self.dcount[key] = 0
        waits = self._deps(eng, reads, writes)
        self.dcount[key] += 16 * n
        tok = (key, self.dcount[key])
        self.ops[eng].append((_freeze(fn), waits, (key, 16)))
        self._commit(tok, reads, writes)
        if is_output:
            self.out_tokens.append(tok)
        return tok

    def finish(self, eng=SP):
        last = {}
        for (s, v) in self.out_tokens:
            last[s] = max(last.get(s, 0), v)
        self.ops[eng].append((None, sorted(last.items()), None))

    def emit(self):
        nc = self.nc
        with nc.Block() as block:
            def run(e, name):
                for (fn, waits, inc) in self.ops[name]:
                    for (s, v) in waits:
                        e.wait_ge(self.semobj[s], v)
                    if fn is None:
                        continue
                    r = fn(e)
                    if isinstance(r, (list, tuple)):
                        for ins in r:
                            ins.then_inc(self.semobj[inc[0]], inc[1])
                    else:
                        r.then_inc(self.semobj[inc[0]], inc[1])

            @block.sync
            def _(e):
                run(e, SP)

            @block.tensor
            def _(e):
                run(e, PE)

            @block.scalar
            def _(e):
                run(e, ACT)

            @block.vector
            def _(e):
                run(e, DVE)

            @block.gpsimd
            def _(e):
                run(e, POOL)


def au(off, nbytes):
    return ["ar%d" % u for u in range(off // 2048, (off + nbytes - 1) // 2048 + 1)]


def build_nc():
    nc = bass.Bass("TRN2", target_bir_lowering=False)

    def din(name, shape, dt=F32):
        return nc.dram_tensor(name, list(shape), dt, kind="ExternalInput").ap()

    def dout(name, shape, dt=F32):
        return nc.dram_tensor(name, list(shape), dt, kind="ExternalOutput").ap()

    x_d = din("x", [NTOK, D])
    p_d = din("p", [2, NTOK, 256])
    ck_d = din("ck", [2, 512, D])
    cv_d = din("cv", [2, 512, D])
    sconv_d = din("sconv", [2, 30, D])
    sffn_d = din("sffn", [2, 2, DFF])
    w_d = {n: din(n, [2, K, N]) for (n, K, N) in WSPEC}
    vecs_d = din("vecs", [128, NV])
    bnear_d = din("bnear", [128, 2 * NH * 256])
    ident_d = din("ident", [128, 128])

    y_o = dout("y", [NTOK, D])
    kp_o = dout("kp", [2, 512, D])
    vp_o = dout("vp", [2, 512, D])
    ks_o = dout("ks", [2, 512, D])
    vs_o = dout("vs", [2, 512, D])
    convp_o = dout("convp", [2, 30, D])
    convs_o = dout("convs", [2, 30, D])
    ffnp_o = dout("ffnp", [2, 2, DFF])
    ffns_o = dout("ffns", [2, 2, DFF])

    ws = {n: nc.dram_tensor("ws_" + n, [2, K, N], BF16, kind="Internal").ap() for (n, K, N) in WSPEC}

    bns = nc.dram_tensor("bns", [2, 128, NH * 256], BF16, kind="Internal").ap()

    with contextlib.ExitStack() as st:
        P = Prog(nc, st)

        def sb(name, shape, dt):
            return st.enter_context(nc.sbuf_tensor(name, list(shape), dt))

        xT = sb("xT", [128, NB, 512], F32)
        hT = sb("hT", [128, NB, 512], BF16)
        uT = sb("uT", [128, NB, 542], BF16)
        kT = [sb("kT%d" % l, [128, NB, 2, 512], BF16) for l in range(2)]
        Vr = [sb("Vr%d" % l, [128, 2, 4, D], BF16) for l in range(2)]
        uhist = [sb("uhist%d" % l, [128, NB, 30], BF16) for l in range(2)]
        uphist = [sb("uphist%d" % l, [128, FB, 2], BF16) for l in range(2)]
        NPT = 3
        PT = [sb("PT%d" % j, [128, 512], BF16) for j in range(NPT)]
        NSLOT = 2
        slots = [sb("slot%d" % j, [128, 4096], BF16) for j in range(NSLOT)]
        pslot = sb("pslot", [128, 2, D], BF16)
        vecs = sb("vecs_sb", [128, NV], F32)
        bnear = sb("bnear_sb", [128, NH * 256], BF16)
        ident_f = sb("ident_f", [128, 128], F32)
        ident_b = sb("ident_b", [128, 128], BF16)
        ones_b = sb("ones_b", [128, 128], BF16)
        ones_f = sb("ones_f", [128, 128], F32)
        one1_b = sb("one1_b", [128, 64], BF16)
        farm = sb("farm", [128, 64], BF16)
        pT = sb("pT", [128, 2, 512], BF16)
        xs = [sb("xs%d" % j, [128, D], F32) for j in range(2)]
        upr = [sb("upr%d" % j, [128, 514], BF16) for j in range(2)]
        sfT = sb("sfT", [128, FB, 2], BF16)
        dgr = [sb("dgr%d" % j, [128, 128], BF16) for j in range(8)]
        dg_i = [0]
        st_mu = sb("st_mu", [128, 512], F32)
        st_tmp = sb("st_tmp", [128, 512], F32)
        st_rstd = sb("st_rstd", [128, 512], F32)
        sq = [sb("sq%d" % j, [128, 512], BF16) for j in range(2)]
        sg = [sb("sg%d" % j, [128, 512], F32) for j in range(2)]
        facc = [sb("facc%d" % j, [128, 512], F32) for j in range(2)]
        tmpb = [sb("tmpb%d" % j, [128, 512], BF16) for j in range(2)]
        smst = sb("smst", [128, 2, 256], F32)
        arena = sb("arena", [128, 20480], BF16)
        banks = [st.enter_context(nc.psum_tensor("bk%d" % j, [128, 512], F32)) for j in range(8)]

        cF = arena[:, 0:8192].bitcast(F32)

        def cFb(b):
            return cF[:, b * 512:(b + 1) * 512], au(b * 2048, 2048)

        def gTb(b):
            return arena[:, b * 512:(b + 1) * 512], au(b * 1024, 1024)

        def mTb(b):
            return arena[:, 4096 + b * 512:4096 + (b + 1) * 512], au(8192 + b * 1024, 1024)

        def sTb(b):
            return arena[:, 8192 + b * 512:8192 + (b + 1) * 512], au(16384 + b * 1024, 1024)

        def qTb(b):
            return arena[:, 12288 + b * 512:12288 + (b + 1) * 512], au(24576 + b * 1024, 1024)

        def aTb(b):
            return arena[:, 16384 + b * 512:16384 + (b + 1) * 512], au(32768 + b * 1024, 1024)

        def fTb(fb):
            return arena[:, fb * 512:(fb + 1) * 512], au(fb * 1024, 1024)

        bank_i = [0]

        def next_bank():
            j = bank_i[0] % 8
            bank_i[0] += 1
            return banks[j], "bk%d" % j

        xs_i = [0]

        def next_xs():
            j = xs_i[0] % 2
            xs_i[0] += 1
            return xs[j], "xs%d" % j

        evac_i = [0]

        def evac_eng():
            evac_i[0] += 1
            return ACT if evac_i[0] % 2 else DVE

        def vcol(name, l, idx):
            c = l * VPL + VOFF[name] + idx
            return vecs[:, c:c + 1]

        def copy_op(eng, out, in_, reads, writes, scale=None):
            if eng == ACT:
                if scale is None:
                    P.op(ACT, lambda e: e.activation(out=out, in_=in_, func=AF.Copy), reads=reads, writes=writes)
                else:
                    P.op(ACT, lambda e: e.activation(out=out, in_=in_, func=AF.Copy, scale=scale), reads=reads, writes=writes)
            else:
                if scale is None:
                    P.op(eng, lambda e: e.tensor_copy(out=out, in_=in_), reads=reads, writes=writes)
                else:
                    P.op(eng, lambda e: e.tensor_scalar(out=out, in0=in_, scalar1=scale, scalar2=None, op0=ALU.mult),
                         reads=reads, writes=writes)

        P.dma(POOL, lambda e: [e.dma_start(out=vecs[:], in_=vecs_d)], "c0", writes=["vecs"])
        P.dma(POOL, lambda e: [e.dma_start(out=ident_f[:], in_=ident_d)], "c1", writes=["ident_f"])
        P.op(DVE, lambda e: e.tensor_copy(out=ident_b[:], in_=ident_f[:]), reads=["ident_f"], writes=["ident_b"])
        P.op(POOL, lambda e: e.memset(ones_b[:], 1.0 / 1024.0), writes=["ones_b"])
        P.op(POOL, lambda e: e.memset(ones_f[:], 1.0 / 1024.0), writes=["ones_f"])
        P.op(POOL, lambda e: e.memset(one1_b[:], 1.0), writes=["one1_b"])
        P.op(POOL, lambda e: e.memset(farm[0:64, :], MASKV), writes=["farm"])
        P.op(POOL, lambda e: e.memset(farm[64:128, :], 0.0), writes=["farm"])
        for l in range(2):
            P.dma(POOL, lambda e, l=l: [e.dma_start(out=bnear[:], in_=bnear_d[:, l * NH * 256:(l + 1) * NH * 256])], "c2",
                  writes=["bnear"])
            for h in range(NH):
                P.op(DVE, lambda e, l=l, h=h: e.tensor_scalar(out=bnear[:, h * 256:(h + 1) * 256], in0=bnear[:, h * 256:(h + 1) * 256],
                                                              scalar1=vecs[:, 2 * VPL + 8 + l * NH + h:2 * VPL + 8 + l * NH + h + 1],
                                                              scalar2=None, op0=ALU.subtract),
                     reads=["bnear", "vecs"], writes=["bnear"])
            P.dma(POOL, lambda e, l=l: [e.dma_start(out=bns[l], in_=bnear[:])], "c3", reads=["bnear"], writes=["bns%d" % l])
        CONSTS = ["vecs", "ident_f", "ident_b", "ones_b", "ones_f", "one1_b", "farm", "bnear"]

        P0_ORDER = ["w_in", "w_conv_out", "w_att_out", "w_out", "w_ffn_up", "w_ffn_down", "w_ple_gate", "w_ple_proj"]
        WDIM = {n: (K, N) for (n, K, N) in WSPEC}
        p0_done = set()

        def p0_gen():
            cnt = 0
            for l in range(2):
                for n in P0_ORDER:
                    K, N = WDIM[n]
                    for c0 in range(0, N, 2048):
                        cc = c0 // 2048
                        for rb in range(K // 128):
                            cw = min(2048, N - c0)
                            nb_ = cw // 512
                            i = cnt % 2
                            stg = Vr[i][:, 1, :, :].rearrange("p a c -> p (a c)").bitcast(F32)
                            ob = kT[i][:, 0:nb_, 1, :]
                            sres = ["V%d_1_%d_%d" % (i, tb, nh) for tb in range(4) for nh in range(2)]
                            ores = ["kT%d_1_%d" % (i, bb) for bb in range(4)]
                            src = w_d[n][l, rb * 128:(rb + 1) * 128, c0:c0 + cw]
                            dst = ws[n][l, rb * 128:(rb + 1) * 128, c0:c0 + cw].rearrange("p (a c) -> p a c", c=512)
                            P.dma(SP, lambda e, o=stg[:, 0:cw], s=src: [e.dma_start(out=o, in_=s)], "p0l%d" % i, writes=sres)
                            eng = [ACT, DVE][cnt % 2]
                            copy_op(eng, ob, stg[:, 0:cw].rearrange("p (a c) -> p a c", c=512), sres, ores)
                            P.dma(ACT, lambda e, o=dst, s=ob: [e.dma_start(out=o, in_=s)], "p0s%d" % i,
                                  reads=ores, writes=["ws_%s_%d_%d_%d" % (n, l, cc, i)])
                            cnt += 1
                            yield
                        p0_done.add((n, l, cc))

        p0 = p0_gen()

        def p0_advance(k):
            for _ in range(k):
                try:
                    next(p0)
                except StopIteration:
                    return

        def p0_ensure(n, l, ccs):
            for cc in ccs:
                while (n, l, cc) not in p0_done:
                    try:
                        next(p0)
                    except StopIteration:
                        return

        def wres(n, l, ccs):
            return ["ws_%s_%d_%d_%d" % (n, l, cc, i) for cc in ccs for i in range(2)]

        slot_i = [0]

        def load_chunk(n, l, pieces):
            ccs = sorted(set(c // 2048 for (c0_, cw_) in pieces for c in (c0_, c0_ + cw_ - 1)))
            p0_ensure(n, l, ccs)
            K = [k for (nn, k, _) in WSPEC if nn == n][0]
            KB = K // 128
            W = sum(c for _, c in pieces)
            j = slot_i[0] % NSLOT
            slot_i[0] += 1
            sl = slots[j][:, 0:KB * W].rearrange("p (k w) -> p k w", w=W)
            srcv = ws[n][l].rearrange("(k p) n -> p k n", p=128)

            def fn(e, sl=sl, srcv=srcv, pieces=pieces):
                r = []
                o = 0
                for (c0, cw) in pieces:
                    r.append(e.dma_start(out=sl[:, :, o:o + cw], in_=srcv[:, :, c0:c0 + cw]))
                    o += cw
                return r
            P.dma(SP, fn, "wl%d" % j, reads=wres(n, l, ccs), writes=["slot%d" % j], n=len(pieces))
            p0_advance(2)
            return sl, "slot%d" % j, KB

        def mm_group(bank, bres, Tn, lhs_list, rhs_list, reads):
            def fn(e):
                r = None
                nk = len(lhs_list)
                for k in range(nk):
                    r = e.matmul(bank, lhs_list[k], rhs_list[k], start=(k == 0), stop=(k == nk - 1))
                return r
            P.op(PE, fn, reads=reads, writes=[bres])

        def rms_stats(Tn, src_fn):
            bk, bres = next_bank()
            for b in range(NB):
                s_ap, s_res = src_fn(b)
                j = b % 2
                P.op(ACT, lambda e, o=sq[j][:, 0:Tn], i_=s_ap: e.activation(out=o, in_=i_, func=AF.Square),
                     reads=s_res, writes=["sq%d" % j])
                P.op(PE, lambda e, b=b, j=j: e.matmul(bk[:, 0:Tn], ones_b[:], sq[j][:, 0:Tn], start=(b == 0), stop=(b == NB - 1)),
                     reads=["sq%d" % j, "ones_b"], writes=[bres])
            P.op(ACT, lambda e: e.activation(out=st_tmp[:, 0:Tn], in_=bk[:, 0:Tn], func=AF.Sqrt, bias=EPS, scale=1.0),
                 reads=[bres], writes=["st_tmp"])
            P.op(DVE, lambda e: e.reciprocal(out=st_rstd[:, 0:Tn], in_=st_tmp[:, 0:Tn]), reads=["st_tmp"], writes=["st_rstd"])

        def rmsnorm_to_hT(Tn, gname, l):
            rms_stats(Tn, lambda b: (xT[:, b, 0:Tn], ["xT%d" % b]))
            for b in range(NB):
                P.op(DVE, lambda e, b=b: e.scalar_tensor_tensor(out=hT[:, b, 0:Tn], in0=xT[:, b, 0:Tn], scalar=vcol(gname, l, b),
                                                                in1=st_rstd[:, 0:Tn], op0=ALU.mult, op1=ALU.mult),
                     reads=["xT%d" % b, "st_rstd", "vecs"], writes=["hT%d" % b])

        def hT_src(Tn):
            return [hT[:, kb, 0:Tn] for kb in range(NB)], ["hT%d" % kb for kb in range(NB)]

        def tm_out(bk, bres, tb, cs, outp, outs, l, eng=None):
            stg, sres = next_xs()
            copy_op(eng or evac_eng(), stg[:, 0:512], bk[:, 0:512], [bres], [sres])
            if not cur["last"]:
                r0 = tb * 128 - 64
                if r0 < 0:
                    P.dma(SP, lambda e: [e.dma_start(out=outp[l, 0:64, cs], in_=stg[64:128, 0:512])],
                          "o_" + sres, reads=[sres], is_output=True)
                else:
                    P.dma(SP, lambda e: [e.dma_start(out=outp[l, r0:r0 + 128, cs], in_=stg[:, 0:512])],
                          "o_" + sres, reads=[sres], is_output=True)
            else:
                P.dma(SP, lambda e: [e.dma_start(out=outp[l, 448:512, cs], in_=stg[0:64, 0:512]),
                                       e.dma_start(out=outs[l, 448:512, cs], in_=stg[64:128, 0:512])],
                      "o_" + sres, reads=[sres], n=2, is_output=True)

        cur = {"last": False}
        try:
          for ti, (t0, Tn) in enumerate(TILES):
              last = (ti == 5)
              cur["last"] = last
              half = ti % 2
              ntb = Tn // 128

              def seg(ap, n, stride):
                  if not last:
                      return ap[:, 0:Tn]
                  return ap[:, 0:2 * stride].rearrange("p (s c) -> p s c", c=stride)[:, :, 0:64]

              def pseg(ap):
                  if not last:
                      return ap[:, 0:Tn]
                  return ap[:, 0:128].rearrange("p (s c) -> p s c", c=64)

              for tb in range(ntb):
                  stg, sres = next_xs()
                  P.dma(POOL, lambda e, o=stg[:], s=x_d[t0 + tb * 128:t0 + (tb + 1) * 128, :]: [e.dma_start(out=o, in_=s)],
                        sres, writes=[sres])
                  for hh in range(2):
                      bk, bres = next_bank()

                      def fn(e, stg=stg, bk=bk, hh=hh):
                          r = None
                          for f in range(4):
                              fb = hh * 4 + f
                              r = e.transpose(bk[:, f * 128:(f + 1) * 128], stg[:, fb * 128:(fb + 1) * 128], ident_f[:])
                          return r
                      P.op(PE, fn, reads=[sres, "ident_f"], writes=[bres])
                      copy_op(evac_eng(), xT[:, hh * 4:hh * 4 + 4, tb * 128:(tb + 1) * 128],
                              bk[:, 0:512].rearrange("p (f c) -> p f c", c=128), [bres], ["xT%d" % (hh * 4 + f) for f in range(4)])

              for l in range(2):
                  for tb in range(ntb):
                      stg, sres = next_xs()
                      P.dma(POOL, lambda e, o=stg[:, 0:256], s=p_d[l, t0 + tb * 128:t0 + (tb + 1) * 128, :]: [e.dma_start(out=o, in_=s)],
                            sres, writes=[sres])
                      bk, bres = next_bank()

                      def fn(e, stg=stg, bk=bk):
                          e.transpose(bk[:, 0:128], stg[:, 0:128], ident_f[:])
                          return e.transpose(bk[:, 128:256], stg[:, 128:256], ident_f[:])
                      P.op(PE, fn, reads=[sres, "ident_f"], writes=[bres])
                      copy_op(evac_eng(), pT[:, 0:2, tb * 128:(tb + 1) * 128], bk[:, 0:256].rearrange("p (f c) -> p f c", c=128),
                              [bres], ["pT"])

                  rmsnorm_to_hT(Tn, "norm_mix", l)
                  hsrc, hres = hT_src(Tn)

                  if ti == 0:
                      P.op(POOL, lambda e: e.memset(uT[:, :, 0:30], 0.0), writes=["uT%d" % b for b in range(NB)])
                  else:
                      P.op(POOL, lambda e, l=l: e.tensor_copy(out=uT[:, :, 0:30], in_=uhist[l][:]),
                           reads=["uhist%d" % l], writes=["uT%d" % b for b in range(NB)])
                  if last:
                      stg, sres = next_xs()
                      P.dma(POOL, lambda e, o=stg[0:30, :], s=sconv_d[l]: [e.dma_start(out=o, in_=s)], sres, writes=[sres])
                      bk, bres = next_bank()

                      def fn(e, stg=stg, bk=bk):
                          r = None
                          for fb in range(NB):
                              r = e.transpose(bk[:, fb * 30:(fb + 1) * 30], stg[0:30, fb * 128:(fb + 1) * 128], ident_f[0:30, 0:30])
                          return r
                      P.op(PE, fn, reads=[sres, "ident_f"], writes=[bres])
                      copy_op(ACT, uT[:, :, 94:124], bk[:, 0:240].rearrange("p (f c) -> p f c", c=30), [bres],
                              ["uT%d" % b for b in range(NB)])

                  chk('t%d_l%d_s2' % (ti, l))
                  if last:
                      utm, utm_res = next_xs()
                  for g in range(4):
                      sl, sres_, KB = load_chunk("w_in", l, [(g * 256, 256), (1024 + g * 256, 256)])
                      for j in range(2):
                          blk = 2 * g + j
                          bA, rA = next_bank()
                          mm_group(bA[:, 0:Tn], rA, Tn, [sl[:, kb, j * 128:(j + 1) * 128] for kb in range(KB)], hsrc, hres + [sres_])
                          bB, rB = next_bank()
                          mm_group(bB[:, 0:Tn], rB, Tn, [sl[:, kb, 256 + j * 128:256 + (j + 1) * 128] for kb in range(KB)], hsrc, hres + [sres_])
                          sj = blk % 2
                          P.op(ACT, lambda e, o=sg[sj][:, 0:Tn], i_=bB[:, 0:Tn]: e.activation(out=o, in_=i_, func=AF.Sigmoid),
                               reads=[rB], writes=["sg%d" % sj])
                          P.op(DVE, lambda e, o=seg(uT[:, blk, 30:542], Tn, 94), a=pseg(bA), s_=pseg(sg[sj]): e.tensor_tensor(out=o, in0=a, in1=s_, op=ALU.mult),
                               reads=[rA, "sg%d" % sj], writes=["uT%d" % blk])
                      if last:
                          bT, rT = next_bank()
                          mm_group(bT[:, 0:512], rT, 512, [hT[:, kb, 0:128] for kb in range(KB)], [sl[:, kb, 0:512] for kb in range(KB)],
                                   hres + [sres_])
                          P.op(ACT, lambda e, i_=bT[:, 256:512]: e.activation(out=sg[0][:, 0:256], in_=i_, func=AF.Sigmoid),
                               reads=[rT], writes=["sg0"])
                          P.op(DVE, lambda e, o=utm[:, g * 256:(g + 1) * 256], a=bT[:, 0:256]: e.tensor_tensor(out=o, in0=a, in1=sg[0][:, 0:256], op=ALU.mult),
                               reads=[rT, "sg0"], writes=[utm_res])
                  if last:
                      P.dma(POOL, lambda e, utm=utm, l=l: [e.dma_start(out=convp_o[l], in_=utm[34:64, :]),
                                                           e.dma_start(out=convs_o[l], in_=utm[98:128, :])],
                            "o_" + utm_res, reads=[utm_res], n=2, is_output=True)

                  chk('t%d_l%d_s3' % (ti, l))
                  want_tm = ti >= 4
                  for which in range(2):
                      for g in range(2):
                          sl, sres_, KB = load_chunk("w_in", l, [(2048 + which * 1024 + g * 512, 512)])
                          for j in range(4):
                              blk = 4 * g + j
                              bk, bres = next_bank()
                              mm_group(bk[:, 0:Tn], bres, Tn, [sl[:, kb, j * 128:(j + 1) * 128] for kb in range(KB)], hsrc, hres + [sres_])
                              if which == 0:
                                  qa, qr = qTb(blk)
                                  copy_op(ACT, qa[:, 0:Tn], bk[:, 0:Tn], [bres], qr, scale=0.125)
                              else:
                                  copy_op(DVE, kT[l][:, blk, half, 0:Tn], bk[:, 0:Tn], [bres], ["kT%d_%d_%d" % (l, half, blk)])
                          if which == 1 and want_tm:
                              for tb in range(ntb):
                                  bT, rT = next_bank()
                                  mm_group(bT[:, 0:512], rT, 512, [hT[:, kb, tb * 128:(tb + 1) * 128] for kb in range(KB)],
                                           [sl[:, kb, 0:512] for kb in range(KB)], hres + [sres_])
                                  tm_out(bT, rT, tb, slice(g * 512, (g + 1) * 512), kp_o, ks_o, l)
                  chk('t%d_l%d_s3v' % (ti, l))
                  for nh in range(2):
                      sl, sres_, KB = load_chunk("w_in", l, [(4096 + nh * 512, 512)])
                      for tb in range(ntb):
                          bk, bres = next_bank()
                          mm_group(bk[:, 0:512], bres, 512, [hT[:, kb, tb * 128:(tb + 1) * 128] for kb in range(KB)],
                                   [sl[:, kb, 0:512] for kb in range(KB)], hres + [sres_])
                          veng = evac_eng()
                          copy_op(veng, Vr[l][:, half, tb, nh * 512:(nh + 1) * 512], bk[:, 0:512], [bres],
                                  ["V%d_%d_%d_%d" % (l, half, tb, nh)])
                          if want_tm:
                              tm_out(bk, bres, tb, slice(nh * 512, (nh + 1) * 512), vp_o, vs_o, l, eng=veng)

                  chk('t%d_l%d_s4' % (ti, l))
                  for b in range(NB):
                      ca, cr = cFb(b)
                      bk, bres = next_bank()
                      for g0 in range(0, 31, 4):
                          taps = list(range(g0, min(31, g0 + 4)))
                          rs = []
                          for j in taps:
                              r = dg_i[0] % 8
                              dg_i[0] += 1
                              rs.append(r)
                              if j % 2 == 0:
                                  P.op(DVE, lambda e, r=r, sc=vcol("conv_dw", l, j * 8 + b): e.tensor_scalar(out=dgr[r][:], in0=ident_b[:], scalar1=sc, scalar2=None, op0=ALU.mult),
                                       reads=["ident_b", "vecs"], writes=["dg%d" % r])
                              else:
                                  P.op(ACT, lambda e, r=r, sc=vcol("conv_dw", l, j * 8 + b): e.activation(out=dgr[r][:], in_=ident_b[:], func=AF.Identity, bias=0.0, scale=sc),
                                       reads=["ident_b", "vecs"], writes=["dg%d" % r])

                          def fn(e, taps=taps, rs=rs, bk=bk, b=b, last=last, Tn=Tn):
                              r_ = None
                              for j, r in zip(taps, rs):
                                  if not last:
                                      r_ = e.matmul(bk[:, 0:Tn], dgr[r][:], uT[:, b, j:j + Tn], start=(j == 0), stop=(j == 30))
                                  else:
                                      r_ = e.matmul(bk[:, 0:64], dgr[r][:], uT[:, b, j:j + 64], start=(j == 0), stop=False, skip_group_check=True)
                                      r_ = e.matmul(bk[:, 64:128], dgr[r][:], uT[:, b, 94 + j:94 + j + 64], start=False, stop=(j == 30), skip_group_check=True)
                              return r_
                          P.op(PE, fn, reads=["uT%d" % b] + ["dg%d" % r for r in rs], writes=[bres])
                      P.op(ACT, lambda e, ca=ca, bk=bk, bia=vcol("conv_dw_b", l, b): e.activation(out=ca[:, 0:Tn], in_=bk[:, 0:Tn], func=AF.Identity, bias=bia, scale=1.0),
                           reads=[bres, "vecs"], writes=cr)
                  if not last:
                      P.op(POOL, lambda e, l=l: e.tensor_copy(out=uhist[l][:], in_=uT[:, :, 512:542]),
                           reads=["uT%d" % b for b in range(NB)], writes=["uhist%d" % l])
                  bmu, rmu = next_bank()
                  bms, rms_ = next_bank()
                  for b in range(NB):
                      ca, cr = cFb(b)
                      P.op(PE, lambda e, b=b, ca=ca: e.matmul(bmu[:, 0:Tn], ones_f[:], ca[:, 0:Tn], start=(b == 0), stop=(b == NB - 1)),
                           reads=cr + ["ones_f"], writes=[rmu])
                      j = b % 2
                      P.op(ACT, lambda e, o=sq[j][:, 0:Tn], i_=ca[:, 0:Tn]: e.activation(out=o, in_=i_, func=AF.Square),
                           reads=cr, writes=["sq%d" % j])
                      P.op(PE, lambda e, b=b, j=j: e.matmul(bms[:, 0:Tn], ones_b[:], sq[j][:, 0:Tn], start=(b == 0), stop=(b == NB - 1)),
                           reads=["sq%d" % j, "ones_b"], writes=[rms_])
                  copy_op(ACT, st_mu[:, 0:Tn], bmu[:, 0:Tn], [rmu], ["st_mu"])
                  P.op(DVE, lambda e: e.tensor_tensor(out=st_tmp[:, 0:Tn], in0=st_mu[:, 0:Tn], in1=st_mu[:, 0:Tn], op=ALU.mult),
                       reads=["st_mu"], writes=["st_tmp"])
                  P.op(DVE, lambda e: e.tensor_tensor(out=st_tmp[:, 0:Tn], in0=bms[:, 0:Tn], in1=st_tmp[:, 0:Tn], op=ALU.subtract),
                       reads=[rms_, "st_tmp"], writes=["st_tmp"])
                  P.op(ACT, lambda e: e.activation(out=st_tmp[:, 0:Tn], in_=st_tmp[:, 0:Tn], func=AF.Sqrt, bias=EPS, scale=1.0),
                       reads=["st_tmp"], writes=["st_tmp"])
                  P.op(DVE, lambda e: e.reciprocal(out=st_rstd[:, 0:Tn], in_=st_tmp[:, 0:Tn]), reads=["st_tmp"], writes=["st_rstd"])
                  for b in range(NB):
                      ca, cr = cFb(b)
                      sa, sr = sTb(b)
                      P.op(DVE, lambda e, ca=ca: e.tensor_tensor(out=ca[:, 0:Tn], in0=ca[:, 0:Tn], in1=st_mu[:, 0:Tn], op=ALU.subtract),
                           reads=cr + ["st_mu"], writes=cr)
                      P.op(DVE, lambda e, ca=ca: e.tensor_tensor(out=ca[:, 0:Tn], in0=ca[:, 0:Tn], in1=st_rstd[:, 0:Tn], op=ALU.mult),
                           reads=cr + ["st_rstd"], writes=cr)
                      P.op(ACT, lambda e, ca=ca, sa=sa, b=b, l=l: e.activation(out=sa[:, 0:Tn], in_=ca[:, 0:Tn], func=AF.Silu,
                                                                              bias=vcol("conv_ln_b", l, b), scale=vcol("conv_ln_g", l, b)),
                           reads=cr + ["vecs"], writes=sr)

                  chk('t%d_l%d_s5' % (ti, l))
                  def gates(which):
                      for g in range(2):
                          sl, sres_, KB = load_chunk("w_in", l, [(5120 + which * 1024 + g * 512, 512)])
                          for j in range(4):
                              blk = 4 * g + j
                              bk, bres = next_bank()
                              mm_group(bk[:, 0:Tn], bres, Tn, [sl[:, kb, j * 128:(j + 1) * 128] for kb in range(KB)], hsrc, hres + [sres_])
                              ga, gr = gTb(blk)
                              P.op(ACT, lambda e, ga=ga, bk=bk, blk=blk: e.activation(out=ga[:, 0:Tn], in_=bk[:, 0:Tn], func=AF.Sigmoid,
                                                                                      bias=vcol("b_gate", l, which * 8 + blk), scale=1.0),
                                   reads=[bres, "vecs"], writes=gr)
                  gates(0)
                  ssrc = [sTb(b)[0][:, 0:Tn] for b in range(NB)]
                  ssres = sum([sTb(b)[1] for b in range(NB)], [])
                  for g in range(2):
                      sl, sres_, KB = load_chunk("w_conv_out", l, [(g * 512, 512)])
                      for j in range(4):
                          blk = 4 * g + j
                          bk, bres = next_bank()
                          mm_group(bk[:, 0:Tn], bres, Tn, [sl[:, kb, j * 128:(j + 1) * 128] for kb in range(KB)], ssrc, ssres + [sres_])
                          ga, gr = gTb(blk)
                          ma, mr = mTb(blk)
                          P.op(DVE, lambda e, ma=ma, bk=bk, ga=ga: e.tensor_tensor(out=ma[:, 0:Tn], in0=bk[:, 0:Tn], in1=ga[:, 0:Tn], op=ALU.mult),
                               reads=[bres] + gr, writes=mr)

                  chk('t%d_l%d_s6' % (ti, l))
                  P.dma(SP, lambda e, l=l: [e.dma_start(out=bnear[:], in_=bns[l])], "bnl", reads=["bns%d" % l], writes=["bnear"])

                  def att_all(qc0, qc1, kblocks, LA=2):
                      units = [(i, kb_, s) for i in range(8) for kb_ in kblocks for s in range(2)]
                      nper = 2 * len(kblocks)
                      info = {}

                      def emit_score(u):
                          i, kb_, s = units[u]
                          r0, r1 = kb_["rows"]
                          c0, c1 = kb_["cols"]
                          qa, qr = qTb(i)
                          h = 2 * i + s
                          j = u % 4
                          j2 = u % NPT
                          Sb, Sres = banks[j], "bk%d" % j
                          kap = kb_["kTf"](s, i)
                          qap = qa[64 * s:64 * s + 64, c0:c1]
                          tb_ = bnear[:, h * 256:(h + 1) * 256]

                          def fs(e, Sb=Sb, kap=kap, qap=qap, kb_=kb_, tb_=tb_, r0=r0, r1=r1, c0=c0, c1=c1):
                              r = e.matmul(Sb[r0:r1, c0:c1], kap, qap, start=True, stop=False, skip_group_check=True)
                              for (tc0, ncol, oc0) in kb_["near"]:
                                  r = e.matmul(Sb[r0:r1, oc0:oc0 + ncol], ident_b[r0:r1, r0:r1], tb_[r0:r1, tc0:tc0 + ncol],
                                               start=False, stop=False, skip_group_check=True)
                              if kb_["far"] is not None:
                                  oc0 = kb_["far"]
                                  r = e.matmul(Sb[r0:r1, oc0:oc0 + 64], ident_b[r0:r1, r0:r1], farm[r0:r1, :],
                                               start=False, stop=False, skip_group_check=True)
                              return r
                          P.op(PE, fs, reads=kb_["kres"](i) + qr + ["ident_b", "bnear", "farm"], writes=[Sres])
                          cf = vecs[r0:r1, 2 * VPL + 8 + l * NH + h:2 * VPL + 8 + l * NH + h + 1]
                          P.op(ACT, lambda e, o=PT[j2][r0:r1, c0:c1], i_=Sb[r0:r1, c0:c1], cf=cf: e.activation(out=o, in_=i_, func=AF.Exp, bias=cf, scale=1.0),
                               reads=[Sres, "vecs"], writes=["PT%d" % j2])

                      def emit_pv(u):
                          i, kb_, s = units[u]
                          r0, r1 = kb_["rows"]
                          c0, c1 = kb_["cols"]
                          h = 2 * i + s
                          j2 = u % NPT
                          Ob, Ores = banks[4 + i % 2], "bk%d" % (4 + i % 2)
                          Db, Dres = banks[6 + i % 2], "bk%d" % (6 + i % 2)
                          fst = (u % nper) < 2
                          vap = kb_["V"][r0:r1, h * 64:(h + 1) * 64]

                          def fpv(e, vap=vap, j=j2, r0=r0, r1=r1, c0=c0, c1=c1, s=s, fst=fst, Ob=Ob, Db=Db):
                              e.matmul(Ob[64 * s:64 * s + 64, c0:c1], vap, PT[j][r0:r1, c0:c1], start=fst, stop=False, skip_group_check=True)
                              return e.matmul(Db[64 * s:64 * s + 64, c0:c1], one1_b[r0:r1, :], PT[j][r0:r1, c0:c1], start=fst, stop=False,
                                              skip_group_check=True)
                          P.op(PE, fpv, reads=["PT%d" % j2, "one1_b"] + kb_["vres"], writes=[Ores, Dres])
                          if (u % nper) == nper - 1:
                              aa, ar = aTb(i)
                              P.op(DVE, lambda e, Db=Db: e.reciprocal(out=st_tmp[:, qc0:qc1], in_=Db[:, qc0:qc1]), reads=[Dres], writes=["st_tmp"])
                              P.op(DVE, lambda e, aa=aa, Ob=Ob: e.tensor_tensor(out=aa[:, qc0:qc1], in0=Ob[:, qc0:qc1], in1=st_tmp[:, qc0:qc1], op=ALU.mult),
                                   reads=[Ores, "st_tmp"], writes=ar)

                      for idx in range(len(units) + LA):
                          if idx - LA >= 0:
                              emit_pv(idx - LA)
                          if idx < len(units):
                              emit_score(idx)

                  def kblock_pair(hf, pb, rows, cols, near, far):
                      return dict(rows=rows, cols=cols, near=near, far=far, kT=None,
                                  kTf=lambda s, i, hf=hf, pb=pb, rows=rows: kT[l][64 * s:64 * s + 64, i, hf, pb * 128 + rows[0]:pb * 128 + rows[1]],
                                  kres=lambda i, hf=hf: ["kT%d_%d_%d" % (l, hf, i)],
                                  V=Vr[l][:, hf, pb, :], vres=["V%d_%d_%d_%d" % (l, hf, pb, nh) for nh in range(2)])

                  if not last:
                      kbl = []
                      for b in range(8):
                          if b < 4 and ti == 0:
                              continue
                          ilo = max(0, 2 * b - 8)
                          ihi = min(7, 2 * b + 1)
                          near = []
                          cs_ = [c for c in range(4) if 0 <= 2 * b - 8 + c <= 7]
                          if b >= 3 and cs_:
                              near.append((cs_[0] * 64, len(cs_) * 64, (2 * b - 8 + cs_[0]) * 64))
                          far = (2 * b + 1) * 64 if b <= 3 else None
                          hf = (1 - half) if b < 4 else half
                          kbl.append(kblock_pair(hf, b % 4, (0, 128), (ilo * 64, (ihi + 1) * 64), near, far))
                      kbl.sort(key=lambda d: -(d["cols"][1] - d["cols"][0]))
                      att_all(0, Tn, kbl)
                  else:
                      kbl = []
                      for b in range(4):
                          near = [(128, 64, 0)] if b == 3 else []
                          kbl.append(kblock_pair(1 - half, b, (0, 128), (0, 64), near, None))
                      kbl.append(kblock_pair(half, 0, (0, 128), (0, 64), [(0, 64, 0)], None))
                      att_all(0, 64, kbl)
                      chk('t%d_l%d_s6b' % (ti, l))
                      oh = 1 - half
                      for tb in range(4):
                          stg, sres = next_xs()
                          P.dma(POOL, lambda e, o=stg[:], s=ck_d[l, tb * 128:(tb + 1) * 128, :]: [e.dma_start(out=o, in_=s)], sres, writes=[sres])
                          for hh in range(2):
                              bk, bres = next_bank()

                              def fn(e, stg=stg, bk=bk, hh=hh):
                                  r = None
                                  for f in range(4):
                                      fb = hh * 4 + f
                                      r = e.transpose(bk[:, f * 128:(f + 1) * 128], stg[:, fb * 128:(fb + 1) * 128], ident_f[:])
                                  return r
                              P.op(PE, fn, reads=[sres, "ident_f"], writes=[bres])
                              copy_op(evac_eng(), kT[l][:, hh * 4:hh * 4 + 4, oh, tb * 128:(tb + 1) * 128],
                                      bk[:, 0:512].rearrange("p (f c) -> p f c", c=128), [bres],
                                      ["kT%d_%d_%d" % (l, oh, hh * 4 + f) for f in range(4)])
                      P.dma(POOL, lambda e, l=l, oh=oh: [e.dma_start(out=Vr[l][:, oh, :, :], in_=cv_d[l].rearrange("(b p) n -> p b n", p=128))],
                            "cvl", writes=["V%d_%d_%d_%d" % (l, oh, tb, nh) for tb in range(4) for nh in range(2)])
                      chk('t%d_l%d_s6c' % (ti, l))
                      P.dma(POOL, lambda e, l=l: [e.dma_start(out=ks_o[l, 0:448, :], in_=ck_d[l, 64:512, :]),
                                                 e.dma_start(out=vs_o[l, 0:448, :], in_=cv_d[l, 64:512, :])], "occ", n=2, is_output=True)
                      chk('t%d_l%d_s6d' % (ti, l))
                      kbl = []
                      for b in range(4):
                          near = [(128, 64, 64)] if b == 3 else []
                          kbl.append(kblock_pair(oh, b, (0, 128), (64, 128), near, None))
                      kbl.append(kblock_pair(half, 0, (0, 128), (64, 128), [(64, 64, 64)], 64))
                      att_all(64, 128, kbl)

                  chk('t%d_l%d_s7' % (ti, l))
                  gates(1)
                  asrc = [aTb(b)[0][:, 0:Tn] for b in range(NB)]
                  asres = sum([aTb(b)[1] for b in range(NB)], [])
                  for g in range(2):
                      sl, sres_, KB = load_chunk("w_att_out", l, [(g * 512, 512)])
                      for j in range(4):
                          blk = 4 * g + j
                          bk, bres = next_bank()
                          mm_group(bk[:, 0:Tn], bres, Tn, [sl[:, kb, j * 128:(j + 1) * 128] for kb in range(KB)], asrc, asres + [sres_])
                          ga, gr = gTb(blk)
                          ma, mr = mTb(blk)
                          tj = blk % 2
                          P.op(DVE, lambda e, bk=bk, ga=ga, tj=tj: e.tensor_tensor(out=tmpb[tj][:, 0:Tn], in0=bk[:, 0:Tn], in1=ga[:, 0:Tn], op=ALU.mult),
                               reads=[bres] + gr, writes=["tmpb%d" % tj])
                          P.op(POOL, lambda e, ma=ma, tj=tj: e.tensor_tensor(out=ma[:, 0:Tn], in0=ma[:, 0:Tn], in1=tmpb[tj][:, 0:Tn], op=ALU.add),
                               reads=["tmpb%d" % tj] + mr, writes=mr)

                  msrc = [mTb(b)[0][:, 0:Tn] for b in range(NB)]
                  msres = sum([mTb(b)[1] for b in range(NB)], [])
                  for g in range(2):
                      sl, sres_, KB = load_chunk("w_out", l, [(g * 512, 512)])
                      for j in range(4):
                          blk = 4 * g + j
                          bk, bres = next_bank()
                          mm_group(bk[:, 0:Tn], bres, Tn, [sl[:, kb, j * 128:(j + 1) * 128] for kb in range(KB)], msrc, msres + [sres_])
                          P.op(DVE, lambda e, bk=bk, blk=blk: e.tensor_tensor(out=xT[:, blk, 0:Tn], in0=bk[:, 0:Tn], in1=xT[:, blk, 0:Tn], op=ALU.add),
                               reads=[bres, "xT%d" % blk], writes=["xT%d" % blk])

                  chk('t%d_l%d_s9' % (ti, l))
                  rmsnorm_to_hT(Tn, "norm_ffn", l)
                  hsrc, hres = hT_src(Tn)
                  if last:
                      bkS, bresS = next_bank()
                      for pi, (c0, cw) in enumerate([(0, 1024), (1024, 1024), (2048, 768)]):
                          stg, sres = next_xs()
                          P.dma(POOL, lambda e, o=stg[0:2, 0:cw], s=sffn_d[l, :, c0:c0 + cw]: [e.dma_start(out=o, in_=s)], sres, writes=[sres])

                          def fn(e, stg=stg, bkS=bkS, pi=pi, cw=cw):
                              r = None
                              for q in range(cw // 128):
                                  fb = pi * 8 + q
                                  r = e.transpose(bkS[:, fb * 2:(fb + 1) * 2], stg[0:2, q * 128:(q + 1) * 128], ident_f[0:2, 0:2])
                              return r
                          P.op(PE, fn, reads=[sres, "ident_f"], writes=[bresS])
                      copy_op(ACT, sfT[:], bkS[:, 0:2 * FB].rearrange("p (f c) -> p f c", c=2), [bresS], ["sfT"])
                  for g in range(11):
                      sl, sres_, KB = load_chunk("w_ffn_up", l, [(g * 256, 256), (DFF + g * 256, 256)])
                      for j in range(2):
                          fb = 2 * g + j
                          fj = fb % 2
                          ur = ["upr%d" % fj]
                          ua = upr[fj]
                          if ti == 0:
                              P.op(POOL, lambda e, ua=ua: e.memset(ua[:, 0:2], 0.0), writes=ur)
                          else:
                              P.op(POOL, lambda e, ua=ua, fb=fb, l=l: e.tensor_copy(out=ua[:, 0:2], in_=uphist[l][:, fb, :]),
                                   reads=["uph%d_%d" % (l, fb)], writes=ur)
                          if last:
                              P.op(POOL, lambda e, ua=ua, fb=fb: e.tensor_copy(out=ua[:, 66:68], in_=sfT[:, fb, :]), reads=["sfT"], writes=ur)
                          bU, rU = next_bank()
                          mm_group(bU[:, 0:Tn], rU, Tn, [sl[:, kb, j * 128:(j + 1) * 128] for kb in range(KB)], hsrc, hres + [sres_])
                          bG, rG = next_bank()
                          mm_group(bG[:, 0:Tn], rG, Tn, [sl[:, kb, 256 + j * 128:256 + (j + 1) * 128] for kb in range(KB)], hsrc, hres + [sres_])
                          fa, fr = fTb(fb)
                          accv = pseg(facc[fj])
                          copy_op(ACT, seg(ua[:, 2:514], Tn, 66), pseg(bU), [rU], ur)
                          P.op(ACT, lambda e, accv=accv, src=pseg(bU), bia=vcol("ffn_dw_b", l, fb), sc=vcol("ffn_dw", l, 2 * FB + fb):
                               e.activation(out=accv, in_=src, func=AF.Identity, bias=bia, scale=sc),
                               reads=[rU, "vecs"], writes=["facc%d" % fj])
                          for tap in (1, 0):
                              P.op(DVE, lambda e, accv=accv, src=seg(ua[:, tap:514], Tn, 66), sc=vcol("ffn_dw", l, tap * FB + fb): e.scalar_tensor_tensor(
                                  out=accv, in0=src, scalar=sc, in1=accv, op0=ALU.mult, op1=ALU.add),
                                  reads=ur + ["vecs", "facc%d" % fj], writes=["facc%d" % fj])
                          if not last:
                              P.op(POOL, lambda e, ua=ua, fb=fb, l=l: e.tensor_copy(out=uphist[l][:, fb, :], in_=ua[:, 512:514]),
                                   reads=ur, writes=["uph%d_%d" % (l, fb)])
                          P.op(ACT, lambda e, o=sg[fj][:, 0:Tn], i_=facc[fj][:, 0:Tn]: e.activation(out=o, in_=i_, func=AF.Gelu),
                               reads=["facc%d" % fj], writes=["sg%d" % fj])
                          P.op(DVE, lambda e, o=fa[:, 0:Tn], a=bG[:, 0:Tn], b_=sg[fj][:, 0:Tn]: e.tensor_tensor(out=o, in0=a, in1=b_, op=ALU.mult),
                               reads=[rG, "sg%d" % fj], writes=fr)
                      if last:
                          bT, rT = next_bank()
                          mm_group(bT[:, 0:256], rT, 256, [hT[:, kb, 0:128] for kb in range(KB)], [sl[:, kb, 0:256] for kb in range(KB)],
                                   hres + [sres_])
                          sm = g % 2
                          copy_op(evac_eng(), smst[:, sm, :], bT[:, 0:256], [rT], ["smst%d" % sm])
                          P.dma(POOL, lambda e, g=g, l=l, sm=sm: [e.dma_start(out=ffnp_o[l, :, g * 256:(g + 1) * 256], in_=smst[62:64, sm, :]),
                                                                 e.dma_start(out=ffns_o[l, :, g * 256:(g + 1) * 256], in_=smst[126:128, sm, :])],
                                "o_smst%d" % sm, reads=["smst%d" % sm], n=2, is_output=True)

                  chk('t%d_l%d_s11' % (ti, l))
                  fsrc = [fTb(fb)[0][:, 0:Tn] for fb in range(FB)]
                  fsres = sorted(set(sum([fTb(fb)[1] for fb in range(FB)], [])))
                  for blk in range(8):
                      sl, sres_, KB = load_chunk("w_ffn_down", l, [(blk * 128, 128)])
                      bk, bres = next_bank()
                      mm_group(bk[:, 0:Tn], bres, Tn, [sl[:, kb, 0:128] for kb in range(KB)], fsrc, fsres + [sres_])
                      P.op(DVE, lambda e, bk=bk, blk=blk: e.tensor_tensor(out=xT[:, blk, 0:Tn], in0=bk[:, 0:Tn], in1=xT[:, blk, 0:Tn], op=ALU.add),
                           reads=[bres, "xT%d" % blk], writes=["xT%d" % blk])

                  chk('t%d_l%d_s12' % (ti, l))
                  rmsnorm_to_hT(Tn, "norm_ple", l)
                  hsrc, hres = hT_src(Tn)
                  p0_ensure("w_ple_proj", l, [0])
                  P.dma(SP, lambda e, l=l: [e.dma_start(out=pslot[:], in_=ws["w_ple_proj"][l].rearrange("(k p) n -> p k n", p=128))],
                        "wlp", reads=wres("w_ple_proj", l, [0]), writes=["pslot"])
                  for g in range(2):
                      sl, sres_, KB = load_chunk("w_ple_gate", l, [(g * 512, 512)])
                      for j in range(4):
                          blk = 4 * g + j
                          bG, rG = next_bank()
                          mm_group(bG[:, 0:Tn], rG, Tn, [sl[:, kb, j * 128:(j + 1) * 128] for kb in range(KB)], hsrc, hres + [sres_])
                          bP, rP = next_bank()
                          mm_group(bP[:, 0:Tn], rP, Tn, [pslot[:, kb, blk * 128:(blk + 1) * 128] for kb in range(2)],
                                   [pT[:, kb, 0:Tn] for kb in range(2)], ["pslot", "pT"])
                          sj = blk % 2
                          P.op(ACT, lambda e, sj=sj, bG=bG: e.activation(out=sg[sj][:, 0:Tn], in_=bG[:, 0:Tn], func=AF.Sigmoid),
                               reads=[rG], writes=["sg%d" % sj])
                          P.op(DVE, lambda e, sj=sj, bP=bP: e.tensor_tensor(out=facc[sj][:, 0:Tn], in0=bP[:, 0:Tn], in1=sg[sj][:, 0:Tn], op=ALU.mult),
                               reads=[rP, "sg%d" % sj], writes=["facc%d" % sj])
                          P.op(DVE, lambda e, sj=sj, blk=blk: e.tensor_tensor(out=xT[:, blk, 0:Tn], in0=facc[sj][:, 0:Tn], in1=xT[:, blk, 0:Tn], op=ALU.add),
                               reads=["facc%d" % sj, "xT%d" % blk], writes=["xT%d" % blk])

              chk('t%d_fin' % ti)
              p0_advance(100000)
              rms_stats(Tn, lambda b: (xT[:, b, 0:Tn], ["xT%d" % b]))
              for b in range(NB):
                  ca, cr = cFb(b)
                  P.op(DVE, lambda e, b=b, ca=ca: e.scalar_tensor_tensor(out=ca[:, 0:Tn], in0=xT[:, b, 0:Tn], scalar=vecs[:, 2 * VPL + b:2 * VPL + b + 1],
                                                                       in1=st_rstd[:, 0:Tn], op0=ALU.mult, op1=ALU.mult),
                       reads=["xT%d" % b, "st_rstd", "vecs"], writes=cr)
              for tb in range(ntb):
                  stg, sres = next_xs()
                  for hh in range(2):
                      bk, bres = next_bank()

                      def fn(e, bk=bk, hh=hh, tb=tb):
                          r = None
                          for f in range(4):
                              ca, _ = cFb(hh * 4 + f)
                              r = e.transpose(bk[:, f * 128:(f + 1) * 128], ca[:, tb * 128:(tb + 1) * 128], ident_f[:])
                          return r
                      P.op(PE, fn, reads=sum([cFb(hh * 4 + f)[1] for f in range(4)], []) + ["ident_f"], writes=[bres])
                      copy_op(evac_eng(), stg[:, hh * 512:(hh + 1) * 512], bk[:, 0:512], [bres], [sres])
                  P.dma(POOL, lambda e, stg=stg, r0=t0 + tb * 128: [e.dma_start(out=y_o[r0:r0 + 128, :], in_=stg[:])],
                        "o_" + sres, reads=[sres], is_output=True)

        except _Stop:
            pass
        P.finish(POOL)
        P.emit()
    return nc


_NC = None


def _lay(v):
    v = np.asarray(v, np.float32)
    lead = v.shape[:-1]
    nb = v.shape[-1] // 128
    v = v.reshape(lead + (nb, 128))
    v = np.moveaxis(v, -1, 0)
    return v.reshape(128, -1)


def kernel(**inp):
    global _NC
    f = lambda k: np.ascontiguousarray(np.asarray(inp[k], np.float32))
    x_prompt, x_sample, p_prompt, p_sample = f("x_prompt"), f("x_sample"), f("p_prompt"), f("p_sample")
    ck, cv, sconv, sffn = f("cache_att_k"), f("cache_att_v"), f("state_conv"), f("state_ffn_conv")
    rel = f("rel_table")

    cols = []
    for l in range(2):
        cols += [_lay(f("norm_mix")[l]), _lay(f("conv_dw")[l]), _lay(f("conv_dw_b")[l]), _lay(f("conv_ln_g")[l]),
                 _lay(f("conv_ln_b")[l]), _lay(f("b_gate")[l]), _lay(f("norm_ffn")[l]), _lay(f("ffn_dw")[l]),
                 _lay(f("ffn_dw_b")[l]), _lay(f("norm_ple")[l])]
    cols.append(_lay(f("norm_final")))
    cfar = np.broadcast_to(rel[:, :, 256].reshape(1, 32), (128, 32))
    cols.append(cfar)
    vecs = np.ascontiguousarray(np.concatenate(cols, axis=1), np.float32)
    assert vecs.shape == (128, NV), vecs.shape

    kl = np.arange(64)[:, None]
    ql = np.arange(64)[None, :]
    bn = np.zeros((2, NH, 128, 256), np.float32)
    for o in range(4):
        idx = np.clip(64 * o + ql - kl, -128, 128) + 128
        Bo = rel[:, :, idx]
        bn[:, :, 0:64, o * 64:(o + 1) * 64] = Bo
        if o < 3:
            bn[:, :, 64:128, (o + 1) * 64:(o + 2) * 64] = Bo
    bn[:, :, 64:128, 0:64] = MASKV
    bn_raw = bn
    bnear = np.ascontiguousarray(np.moveaxis(bn_raw, 2, 0).reshape(128, 2 * NH * 256))
    ident = np.eye(128, dtype=np.float32)

    in_maps = []
    for c in range(8):
        s, hb = c // 2, c % 2
        ch0 = 0 if hb == 0 else 64 - NCHK
        xs_ = np.concatenate([x_prompt[s, ch0 * 64:(ch0 + NCHK) * 64], x_sample[c]], axis=0)
        ps_ = np.concatenate([p_prompt[:, s, ch0 * 64:(ch0 + NCHK) * 64], p_sample[:, c]], axis=1)
        m = {"x": np.ascontiguousarray(xs_), "p": np.ascontiguousarray(ps_),
             "ck": np.ascontiguousarray(ck[:, c].reshape(2, 512, D)), "cv": np.ascontiguousarray(cv[:, c].reshape(2, 512, D)),
             "sconv": np.ascontiguousarray(sconv[:, c]), "sffn": np.ascontiguousarray(sffn[:, c]),
             "vecs": vecs, "bnear": bnear, "ident": ident}
        for (n, K, N) in WSPEC:
            m[n] = f(n)
        in_maps.append(m)

    if _NC is None:
        _NC = build_nc()
    res = run_bass_kernel_spmd(_NC, in_maps, core_ids=list(range(8)))
    R = res.results

    y_prompt = np.zeros((4, 4096, D), np.float32)
    y_sample = np.zeros((8, 64, D), np.float32)
    nkp = np.zeros((2, 4, 512, NH, 64), np.float32)
    nvp = np.zeros_like(nkp)
    ncp = np.zeros((2, 4, 30, D), np.float32)
    nfp = np.zeros((2, 4, 2, DFF), np.float32)
    nks = np.zeros((2, 8, 512, NH, 64), np.float32)
    nvs = np.zeros_like(nks)
    ncs = np.zeros((2, 8, 30, D), np.float32)
    nfs = np.zeros((2, 8, 2, DFF), np.float32)
    for c in range(8):
        s, hb = c // 2, c % 2
        r = R[c]
        if hb == 0:
            y_prompt[s, 0:NCHK * 64] = r["y"][0:NCHK * 64]
        else:
            y_prompt[s, NCHK * 64:] = r["y"][HALO * 64:NCHK * 64]
            nkp[:, s] = r["kp"].reshape(2, 512, NH, 64)
            nvp[:, s] = r["vp"].reshape(2, 512, NH, 64)
            ncp[:, s] = r["convp"]
            nfp[:, s] = r["ffnp"]
        y_sample[c] = r["y"][NCHK * 64:]
        nks[:, c] = r["ks"].reshape(2, 512, NH, 64)
        nvs[:, c] = r["vs"].reshape(2, 512, NH, 64)
        ncs[:, c] = r["convs"]
        nfs[:, c] = r["ffns"]
    return (y_prompt, y_sample, nkp, nvp, ncp, nfp, nks, nvs, ncs, nfs)
```

```python
import contextlib
import os
import types
import numpy as np
import concourse.bass as bass
import concourse.mybir as mybir
from concourse.bass_utils import run_bass_kernel_spmd

F32 = mybir.dt.float32
BF16 = mybir.dt.bfloat16
AF = mybir.ActivationFunctionType
ALU = mybir.AluOpType
PE, ACT, DVE, POOL, SP = "pe", "act", "dve", "pool", "sp"

D = 1024
NB = 8
NH = 16
DFF = 2816
FB = 22
NIN = 7168
NCHK = 41
HALO = 18
NTOK = NCHK * 64 + 64
TILES = [(t * 512, 512) for t in range(5)] + [(2560, 128)]
EPS = 1e-6
MASKV = -30000.0

VOFF = {}
_c = 0
for _n, _w in [("norm_mix", 8), ("conv_dw", 31 * 8), ("conv_dw_b", 8), ("conv_ln_g", 8), ("conv_ln_b", 8),
               ("b_gate", 16), ("norm_ffn", 8), ("ffn_dw", 3 * 22), ("ffn_dw_b", 22), ("norm_ple", 8)]:
    VOFF[_n] = _c
    _c += _w
VPL = _c
NV = 2 * VPL + 8 + 32

WSPEC = [("w_in", 1024, NIN), ("w_conv_out", 1024, 1024), ("w_att_out", 1024, 1024), ("w_out", 1024, 1024),
         ("w_ffn_up", 1024, 2 * DFF), ("w_ffn_down", DFF, 1024), ("w_ple_gate", 1024, 1024),
         ("w_ple_proj", 256, 1024)]


def _freeze(fn, depth=0):
    if not isinstance(fn, types.FunctionType) or fn.__closure__ is None or depth > 4:
        return fn
    cells = []
    for c in fn.__closure__:
        try:
            v = c.cell_contents
        except ValueError:
            cells.append(c)
            continue
        if isinstance(v, types.FunctionType) and v.__closure__ is not None:
            v = _freeze(v, depth + 1)
        cells.append(types.CellType(v))
    g = types.FunctionType(fn.__code__, fn.__globals__, fn.__name__, fn.__defaults__, tuple(cells))
    g.__kwdefaults__ = fn.__kwdefaults__
    return g


class _Stop(Exception):
    pass


KSTOP = os.environ.get("KSTOP", "")


def chk(tag):
    if KSTOP and tag == KSTOP:
        raise _Stop()


class Prog:
    def __init__(self, nc, stack):
        self.nc = nc
        self.stack = stack
        self.engs = [PE, ACT, DVE, POOL, SP]
        self.ops = {e: [] for e in self.engs}
        self.semobj = {}
        for e in self.engs:
            self.semobj["es_" + e] = stack.enter_context(nc.semaphore("es_" + e))
        self.ecount = {e: 0 for e in self.engs}
        self.waited = {e: {} for e in self.engs}
        self.res = {}
        self.dcount = {}
        self.out_tokens = []

    def _deps(self, eng, reads, writes):
        deps = set()
        for r in reads:
            st = self.res.get(r)
            if st and st["w"]:
                deps.add(st["w"])
        for w in writes:
            st = self.res.get(w)
            if st:
                if st["w"]:
                    deps.add(st["w"])
                deps |= st["r"]
        waits = []
        wd = self.waited[eng]
        own = "es_" + eng
        best = {}
        for (sname, val) in deps:
            best[sname] = max(best.get(sname, 0), val)
        for (sname, val) in sorted(best.items()):
            if eng == PE and sname == own:
                continue
            if wd.get(sname, 0) >= val:
                continue
            wd[sname] = val
            waits.append((sname, val))
        return waits

    def _commit(self, tok, reads, writes):
        for r in reads:
            st = self.res.setdefault(r, {"w": None, "r": set()})
            st["r"].add(tok)
        for w in writes:
            st = self.res.setdefault(w, {"w": None, "r": set()})
            st["w"] = tok
            st["r"] = set()

    def op(self, eng, fn, reads=(), writes=()):
        waits = self._deps(eng, reads, writes)
        self.ecount[eng] += 1
        tok = ("es_" + eng, self.ecount[eng])
        self.ops[eng].append((_freeze(fn), waits, ("es_" + eng, 1)))
        self._commit(tok, reads, writes)
        return tok

    def dma(self, eng, fn, sem, reads=(), writes=(), n=1, is_output=False):
        key = "ds_" + sem
        if key not in self.semobj:
            self.semobj[key] = self.stack.enter_context(self.nc.semaphore(key))
            self.dcount[key] = 0
        waits = self._deps(eng, reads, writes)
        self.dcount[key] += 16 * n
        tok = (key, self.dcount[key])
        self.ops[eng].append((_freeze(fn), waits, (key, 16)))
        self._commit(tok, reads, writes)
        if is_output:
            self.out_tokens.append(tok)
        return tok

    def finish(self, eng=SP):
        last = {}
        for (s, v) in self.out_tokens:
            last[s] = max(last.get(s, 0), v)
        self.ops[eng].append((None, sorted(last.items()), None))

    def emit(self):
        nc = self.nc
        with nc.Block() as block:
            def run(e, name):
                for (fn, waits, inc) in self.ops[name]:
                    for (s, v) in waits:
                        e.wait_ge(self.semobj[s], v)
                    if fn is None:
                        continue
                    r = fn(e)
                    if isinstance(r, (list, tuple)):
                        for ins in r:
                            ins.then_inc(self.semobj[inc[0]], inc[1])
                    else:
                        r.then_inc(self.semobj[inc[0]], inc[1])

            @block.sync
            def _(e):
                run(e, SP)

            @block.tensor
            def _(e):
                run(e, PE)

            @block.scalar
            def _(e):
                run(e, ACT)

            @block.vector
            def _(e):
                run(e, DVE)

            @block.gpsimd
            def _(e):
                run(e, POOL)


def au(off, nbytes):
    return ["ar%d" % u for u in range(off // 2048, (off + nbytes - 1) // 2048 + 1)]


def build_nc():
    nc = bass.Bass("TRN2", target_bir_lowering=False)

    def din(name, shape, dt=F32):
        return nc.dram_tensor(name, list(shape), dt, kind="ExternalInput").ap()

    def dout(name, shape, dt=F32):
        return nc.dram_tensor(name, list(shape), dt, kind="ExternalOutput").ap()

    x_d = din("x", [NTOK, D])
    p_d = din("p", [2, NTOK, 256])
    ck_d = din("ck", [2, 512, D])
    cv_d = din("cv", [2, 512, D])
    sconv_d = din("sconv", [2, 30, D])
    sffn_d = din("sffn", [2, 2, DFF])
    w_d = {n: din(n, [2, K, N]) for (n, K, N) in WSPEC}
    vecs_d = din("vecs", [128, NV])
    bnear_d = din("bnear", [128, 2 * NH * 256])
    ident_d = din("ident", [128, 128])

    y_o = dout("y", [NTOK, D])
    kp_o = dout("kp", [2, 512, D])
    vp_o = dout("vp", [2, 512, D])
    ks_o = dout("ks", [2, 512, D])
    vs_o = dout("vs", [2, 512, D])
    convp_o = dout("convp", [2, 30, D])
    convs_o = dout("convs", [2, 30, D])
    ffnp_o = dout("ffnp", [2, 2, DFF])
    ffns_o = dout("ffns", [2, 2, DFF])

    ws = {n: nc.dram_tensor("ws_" + n, [2, K, N], BF16, kind="Internal").ap() for (n, K, N) in WSPEC}

    bns = nc.dram_tensor("bns", [2, 128, NH * 256], BF16, kind="Internal").ap()

    with contextlib.ExitStack() as st:
        P = Prog(nc, st)

        def sb(name, shape, dt):
            return st.enter_context(nc.sbuf_tensor(name, list(shape), dt))

        xT = sb("xT", [128, NB, 512], F32)
        hT = sb("hT", [128, NB, 512], BF16)
        uT = sb("uT", [128, NB, 542], BF16)
        kT = [sb("kT%d" % l, [128, NB, 2, 512], BF16) for l in range(2)]
        Vr = [sb("Vr%d" % l, [128, 2, 4, D], BF16) for l in range(2)]
        uhist = [sb("uhist%d" % l, [128, NB, 30], BF16) for l in range(2)]
        uphist = [sb("uphist%d" % l, [128, FB, 2], BF16) for l in range(2)]
        NPT = 3
        PT = [sb("PT%d" % j, [128, 512], BF16) for j in range(NPT)]
        NSLOT = 2
        slots = [sb("slot%d" % j, [128, 4096], BF16) for j in range(NSLOT)]
        pslot = sb("pslot", [128, 2, D], BF16)
        vecs = sb("vecs_sb", [128, NV], F32)
        bnear = sb("bnear_sb", [128, NH * 256], BF16)
        ident_f = sb("ident_f", [128, 128], F32)
        ident_b = sb("ident_b", [128, 128], BF16)
        ones_b = sb("ones_b", [128, 128], BF16)
        ones_f = sb("ones_f", [128, 128], F32)
        one1_b = sb("one1_b", [128, 64], BF16)
        farm = sb("farm", [128, 64], BF16)
        pT = sb("pT", [128, 2, 512], BF16)
        xs = [sb("xs%d" % j, [128, D], F32) for j in range(2)]
        upr = [sb("upr%d" % j, [128, 514], BF16) for j in range(2)]
        sfT = sb("sfT", [128, FB, 2], BF16)
        dgr = [sb("dgr%d" % j, [128, 128], BF16) for j in range(8)]
        dg_i = [0]
        st_mu = sb("st_mu", [128, 512], F32)
        st_tmp = sb("st_tmp", [128, 512], F32)
        st_rstd = sb("st_rstd", [128, 512], F32)
        sq = [sb("sq%d" % j, [128, 512], BF16) for j in range(2)]
        sg = [sb("sg%d" % j, [128, 512], F32) for j in range(2)]
        facc = [sb("facc%d" % j, [128, 512], F32) for j in range(2)]
        tmpb = [sb("tmpb%d" % j, [128, 512], BF16) for j in range(2)]
        smst = sb("smst", [128, 2, 256], F32)
        arena = sb("arena", [128, 20480], BF16)
        banks = [st.enter_context(nc.psum_tensor("bk%d" % j, [128, 512], F32)) for j in range(8)]

        cF = arena[:, 0:8192].bitcast(F32)

        def cFb(b):
            return cF[:, b * 512:(b + 1) * 512], au(b * 2048, 2048)

        def gTb(b):
            return arena[:, b * 512:(b + 1) * 512], au(b * 1024, 1024)

        def mTb(b):
            return arena[:, 4096 + b * 512:4096 + (b + 1) * 512], au(8192 + b * 1024, 1024)

        def sTb(b):
            return arena[:, 8192 + b * 512:8192 + (b + 1) * 512], au(16384 + b * 1024, 1024)

        def qTb(b):
            return arena[:, 12288 + b * 512:12288 + (b + 1) * 512], au(24576 + b * 1024, 1024)

        def aTb(b):
            return arena[:, 16384 + b * 512:16384 + (b + 1) * 512], au(32768 + b * 1024, 1024)

        def fTb(fb):
            return arena[:, fb * 512:(fb + 1) * 512], au(fb * 1024, 1024)

        bank_i = [0]

        def next_bank():
            j = bank_i[0] % 8
            bank_i[0] += 1
            return banks[j], "bk%d" % j

        xs_i = [0]

        def next_xs():
            j = xs_i[0] % 2
            xs_i[0] += 1
            return xs[j], "xs%d" % j

        evac_i = [0]

        def evac_eng():
            evac_i[0] += 1
            return ACT if evac_i[0] % 2 else DVE

        def vcol(name, l, idx):
            c = l * VPL + VOFF[name] + idx
            return vecs[:, c:c + 1]

        def copy_op(eng, out, in_, reads, writes, scale=None):
            if eng == ACT:
                if scale is None:
                    P.op(ACT, lambda e: e.activation(out=out, in_=in_, func=AF.Copy), reads=reads, writes=writes)
                else:
                    P.op(ACT, lambda e: e.activation(out=out, in_=in_, func=AF.Copy, scale=scale), reads=reads, writes=writes)
            else:
                if scale is None:
                    P.op(eng, lambda e: e.tensor_copy(out=out, in_=in_), reads=reads, writes=writes)
                else:
                    P.op(eng, lambda e: e.tensor_scalar(out=out, in0=in_, scalar1=scale, scalar2=None, op0=ALU.mult),
                         reads=reads, writes=writes)

        P.dma(POOL, lambda e: [e.dma_start(out=vecs[:], in_=vecs_d)], "c0", writes=["vecs"])
        P.dma(POOL, lambda e: [e.dma_start(out=ident_f[:], in_=ident_d)], "c1", writes=["ident_f"])
        P.op(DVE, lambda e: e.tensor_copy(out=ident_b[:], in_=ident_f[:]), reads=["ident_f"], writes=["ident_b"])
        P.op(POOL, lambda e: e.memset(ones_b[:], 1.0 / 1024.0), writes=["ones_b"])
        P.op(POOL, lambda e: e.memset(ones_f[:], 1.0 / 1024.0), writes=["ones_f"])
        P.op(POOL, lambda e: e.memset(one1_b[:], 1.0), writes=["one1_b"])
        P.op(POOL, lambda e: e.memset(farm[0:64, :], MASKV), writes=["farm"])
        P.op(POOL, lambda e: e.memset(farm[64:128, :], 0.0), writes=["farm"])
        for l in range(2):
            P.dma(POOL, lambda e, l=l: [e.dma_start(out=bnear[:], in_=bnear_d[:, l * NH * 256:(l + 1) * NH * 256])], "c2",
                  writes=["bnear"])
            for h in range(NH):
                P.op(DVE, lambda e, l=l, h=h: e.tensor_scalar(out=bnear[:, h * 256:(h + 1) * 256], in0=bnear[:, h * 256:(h + 1) * 256],
                                                              scalar1=vecs[:, 2 * VPL + 8 + l * NH + h:2 * VPL + 8 + l * NH + h + 1],
                                                              scalar2=None, op0=ALU.subtract),
                     reads=["bnear", "vecs"], writes=["bnear"])
            P.dma(POOL, lambda e, l=l: [e.dma_start(out=bns[l], in_=bnear[:])], "c3", reads=["bnear"], writes=["bns%d" % l])
        CONSTS = ["vecs", "ident_f", "ident_b", "ones_b", "ones_f", "one1_b", "farm", "bnear"]

        P0_ORDER = ["w_in", "w_conv_out", "w_att_out", "w_out", "w_ffn_up", "w_ffn_down", "w_ple_gate", "w_ple_proj"]
        WDIM = {n: (K, N) for (n, K, N) in WSPEC}
        p0_done = set()

        def p0_gen():
            cnt = 0
            for l in range(2):
                for n in P0_ORDER:
                    K, N = WDIM[n]
                    for c0 in range(0, N, 2048):
                        cc = c0 // 2048
                        for rb in range(K // 128):
                            cw = min(2048, N - c0)
                            nb_ = cw // 512
                            i = cnt % 2
                            stg = Vr[i][:, 1, :, :].rearrange("p a c -> p (a c)").bitcast(F32)
                            ob = kT[i][:, 0:nb_, 1, :]
                            sres = ["V%d_1_%d_%d" % (i, tb, nh) for tb in range(4) for nh in range(2)]
                            ores = ["kT%d_1_%d" % (i, bb) for bb in range(4)]
                            src = w_d[n][l, rb * 128:(rb + 1) * 128, c0:c0 + cw]
                            dst = ws[n][l, rb * 128:(rb + 1) * 128, c0:c0 + cw].rearrange("p (a c) -> p a c", c=512)
                            P.dma(SP, lambda e, o=stg[:, 0:cw], s=src: [e.dma_start(out=o, in_=s)], "p0l%d" % i, writes=sres)
                            eng = [ACT, DVE][cnt % 2]
                            copy_op(eng, ob, stg[:, 0:cw].rearrange("p (a c) -> p a c", c=512), sres, ores)
                            P.dma(ACT, lambda e, o=dst, s=ob: [e.dma_start(out=o, in_=s)], "p0s%d" % i,
                                  reads=ores, writes=["ws_%s_%d_%d_%d" % (n, l, cc, i)])
                            cnt += 1
                            yield
                        p0_done.add((n, l, cc))

        p0 = p0_gen()

        def p0_advance(k):
            for _ in range(k):
                try:
                    next(p0)
                except StopIteration:
                    return

        def p0_ensure(n, l, ccs):
            for cc in ccs:
                while (n, l, cc) not in p0_done:
                    try:
                        next(p0)
                    except StopIteration:
                        return

        def wres(n, l, ccs):
            return ["ws_%s_%d_%d_%d" % (n, l, cc, i) for cc in ccs for i in range(2)]

        slot_i = [0]

        def load_chunk(n, l, pieces):
            ccs = sorted(set(c // 2048 for (c0_, cw_) in pieces for c in (c0_, c0_ + cw_ - 1)))
            p0_ensure(n, l, ccs)
            K = [k for (nn, k, _) in WSPEC if nn == n][0]
            KB = K // 128
            W = sum(c for _, c in pieces)
            j = slot_i[0] % NSLOT
            slot_i[0] += 1
            sl = slots[j][:, 0:KB * W].rearrange("p (k w) -> p k w", w=W)
            srcv = ws[n][l].rearrange("(k p) n -> p k n", p=128)

            def fn(e, sl=sl, srcv=srcv, pieces=pieces):
                r = []
                o = 0
                for (c0, cw) in pieces:
                    r.append(e.dma_start(out=sl[:, :, o:o + cw], in_=srcv[:, :, c0:c0 + cw]))
                    o += cw
                return r
            P.dma(SP, fn, "wl%d" % j, reads=wres(n, l, ccs), writes=["slot%d" % j], n=len(pieces))
            p0_advance(2)
            return sl, "slot%d" % j, KB

        def mm_group(bank, bres, Tn, lhs_list, rhs_list, reads):
            def fn(e):
                r = None
                nk = len(lhs_list)
                for k in range(nk):
                    r = e.matmul(bank, lhs_list[k], rhs_list[k], start=(k == 0), stop=(k == nk - 1))
                return r
            P.op(PE, fn, reads=reads, writes=[bres])

        def rms_stats(Tn, src_fn):
            bk, bres = next_bank()
            for b in range(NB):
                s_ap, s_res = src_fn(b)
                j = b % 2
                P.op(ACT, lambda e, o=sq[j][:, 0:Tn], i_=s_ap: e.activation(out=o, in_=i_, func=AF.Square),
                     reads=s_res, writes=["sq%d" % j])
                P.op(PE, lambda e, b=b, j=j: e.matmul(bk[:, 0:Tn], ones_b[:], sq[j][:, 0:Tn], start=(b == 0), stop=(b == NB - 1)),
                     reads=["sq%d" % j, "ones_b"], writes=[bres])
            P.op(ACT, lambda e: e.activation(out=st_tmp[:, 0:Tn], in_=bk[:, 0:Tn], func=AF.Sqrt, bias=EPS, scale=1.0),
                 reads=[bres], writes=["st_tmp"])
            P.op(DVE, lambda e: e.reciprocal(out=st_rstd[:, 0:Tn], in_=st_tmp[:, 0:Tn]), reads=["st_tmp"], writes=["st_rstd"])

        def rmsnorm_to_hT(Tn, gname, l):
            rms_stats(Tn, lambda b: (xT[:, b, 0:Tn], ["xT%d" % b]))
            for b in range(NB):
                P.op(DVE, lambda e, b=b: e.scalar_tensor_tensor(out=hT[:, b, 0:Tn], in0=xT[:, b, 0:Tn], scalar=vcol(gname, l, b),
                                                                in1=st_rstd[:, 0:Tn], op0=ALU.mult, op1=ALU.mult),
                     reads=["xT%d" % b, "st_rstd", "vecs"], writes=["hT%d" % b])

        def hT_src(Tn):
            return [hT[:, kb, 0:Tn] for kb in range(NB)], ["hT%d" % kb for kb in range(NB)]

        def tm_out(bk, bres, tb, cs, outp, outs, l, eng=None):
            stg, sres = next_xs()
            copy_op(eng or evac_eng(), stg[:, 0:512], bk[:, 0:512], [bres], [sres])
            if not cur["last"]:
                r0 = tb * 128 - 64
                if r0 < 0:
                    P.dma(SP, lambda e: [e.dma_start(out=outp[l, 0:64, cs], in_=stg[64:128, 0:512])],
                          "o_" + sres, reads=[sres], is_output=True)
                else:
                    P.dma(SP, lambda e: [e.dma_start(out=outp[l, r0:r0 + 128, cs], in_=stg[:, 0:512])],
                          "o_" + sres, reads=[sres], is_output=True)
            else:
                P.dma(SP, lambda e: [e.dma_start(out=outp[l, 448:512, cs], in_=stg[0:64, 0:512]),
                                       e.dma_start(out=outs[l, 448:512, cs], in_=stg[64:128, 0:512])],
                      "o_" + sres, reads=[sres], n=2, is_output=True)

        cur = {"last": False}
        try:
          for ti, (t0, Tn) in enumerate(TILES):
              last = (ti == 5)
              cur["last"] = last
              half = ti % 2
              ntb = Tn // 128

              def seg(ap, n, stride):
                  if not last:
                      return ap[:, 0:Tn]
                  return ap[:, 0:2 * stride].rearrange("p (s c) -> p s c", c=stride)[:, :, 0:64]

              def pseg(ap):
                  if not last:
                      return ap[:, 0:Tn]
                  return ap[:, 0:128].rearrange("p (s c) -> p s c", c=64)

              for tb in range(ntb):
                  stg, sres = next_xs()
                  P.dma(POOL, lambda e, o=stg[:], s=x_d[t0 + tb * 128:t0 + (tb + 1) * 128, :]: [e.dma_start(out=o, in_=s)],
                        sres, writes=[sres])
                  for hh in range(2):
                      bk, bres = next_bank()

                      def fn(e, stg=stg, bk=bk, hh=hh):
                          r = None
                          for f in range(4):
                              fb = hh * 4 + f
                              r = e.transpose(bk[:, f * 128:(f + 1) * 128], stg[:, fb * 128:(fb + 1) * 128], ident_f[:])
                          return r
                      P.op(PE, fn, reads=[sres, "ident_f"], writes=[bres])
                      copy_op(evac_eng(), xT[:, hh * 4:hh * 4 + 4, tb * 128:(tb + 1) * 128],
                              bk[:, 0:512].rearrange("p (f c) -> p f c", c=128), [bres], ["xT%d" % (hh * 4 + f) for f in range(4)])

              for l in range(2):
                  for tb in range(ntb):
                      stg, sres = next_xs()
                      P.dma(POOL, lambda e, o=stg[:, 0:256], s=p_d[l, t0 + tb * 128:t0 + (tb + 1) * 128, :]: [e.dma_start(out=o, in_=s)],
                            sres, writes=[sres])
                      bk, bres = next_bank()

                      def fn(e, stg=stg, bk=bk):
                          e.transpose(bk[:, 0:128], stg[:, 0:128], ident_f[:])
                          return e.transpose(bk[:, 128:256], stg[:, 128:256], ident_f[:])
                      P.op(PE, fn, reads=[sres, "ident_f"], writes=[bres])
                      copy_op(evac_eng(), pT[:, 0:2, tb * 128:(tb + 1) * 128], bk[:, 0:256].rearrange("p (f c) -> p f c", c=128),
                              [bres], ["pT"])

                  rmsnorm_to_hT(Tn, "norm_mix", l)
                  hsrc, hres = hT_src(Tn)

                  if ti == 0:
                      P.op(POOL, lambda e: e.memset(uT[:, :, 0:30], 0.0), writes=["uT%d" % b for b in range(NB)])
                  else:
                      P.op(POOL, lambda e, l=l: e.tensor_copy(out=uT[:, :, 0:30], in_=uhist[l][:]),
                           reads=["uhist%d" % l], writes=["uT%d" % b for b in range(NB)])
                  if last:
                      stg, sres = next_xs()
                      P.dma(POOL, lambda e, o=stg[0:30, :], s=sconv_d[l]: [e.dma_start(out=o, in_=s)], sres, writes=[sres])
                      bk, bres = next_bank()

                      def fn(e, stg=stg, bk=bk):
                          r = None
                          for fb in range(NB):
                              r = e.transpose(bk[:, fb * 30:(fb + 1) * 30], stg[0:30, fb * 128:(fb + 1) * 128], ident_f[0:30, 0:30])
                          return r
                      P.op(PE, fn, reads=[sres, "ident_f"], writes=[bres])
                      copy_op(ACT, uT[:, :, 94:124], bk[:, 0:240].rearrange("p (f c) -> p f c", c=30), [bres],
                              ["uT%d" % b for b in range(NB)])

                  chk('t%d_l%d_s2' % (ti, l))
                  if last:
                      utm, utm_res = next_xs()
                  for g in range(4):
                      sl, sres_, KB = load_chunk("w_in", l, [(g * 256, 256), (1024 + g * 256, 256)])
                      for j in range(2):
                          blk = 2 * g + j
                          bA, rA = next_bank()
                          mm_group(bA[:, 0:Tn], rA, Tn, [sl[:, kb, j * 128:(j + 1) * 128] for kb in range(KB)], hsrc, hres + [sres_])
                          bB, rB = next_bank()
                          mm_group(bB[:, 0:Tn], rB, Tn, [sl[:, kb, 256 + j * 128:256 + (j + 1) * 128] for kb in range(KB)], hsrc, hres + [sres_])
                          sj = blk % 2
                          P.op(ACT, lambda e, o=sg[sj][:, 0:Tn], i_=bB[:, 0:Tn]: e.activation(out=o, in_=i_, func=AF.Sigmoid),
                               reads=[rB], writes=["sg%d" % sj])
                          P.op(DVE, lambda e, o=seg(uT[:, blk, 30:542], Tn, 94), a=pseg(bA), s_=pseg(sg[sj]): e.tensor_tensor(out=o, in0=a, in1=s_, op=ALU.mult),
                               reads=[rA, "sg%d" % sj], writes=["uT%d" % blk])
                      if last:
                          bT, rT = next_bank()
                          mm_group(bT[:, 0:512], rT, 512, [hT[:, kb, 0:128] for kb in range(KB)], [sl[:, kb, 0:512] for kb in range(KB)],
                                   hres + [sres_])
                          P.op(ACT, lambda e, i_=bT[:, 256:512]: e.activation(out=sg[0][:, 0:256], in_=i_, func=AF.Sigmoid),
                               reads=[rT], writes=["sg0"])
                          P.op(DVE, lambda e, o=utm[:, g * 256:(g + 1) * 256], a=bT[:, 0:256]: e.tensor_tensor(out=o, in0=a, in1=sg[0][:, 0:256], op=ALU.mult),
                               reads=[rT, "sg0"], writes=[utm_res])
                  if last:
                      P.dma(POOL, lambda e, utm=utm, l=l: [e.dma_start(out=convp_o[l], in_=utm[34:64, :]),
                                                           e.dma_start(out=convs_o[l], in_=utm[98:128, :])],
                            "o_" + utm_res, reads=[utm_res], n=2, is_output=True)

                  chk('t%d_l%d_s3' % (ti, l))
                  want_tm = ti >= 4
                  for which in range(2):
                      for g in range(2):
                          sl, sres_, KB = load_chunk("w_in", l, [(2048 + which * 1024 + g * 512, 512)])
                          for j in range(4):
                              blk = 4 * g + j
                              bk, bres = next_bank()
                              mm_group(bk[:, 0:Tn], bres, Tn, [sl[:, kb, j * 128:(j + 1) * 128] for kb in range(KB)], hsrc, hres + [sres_])
                              if which == 0:
                                  qa, qr = qTb(blk)
                                  copy_op(ACT, qa[:, 0:Tn], bk[:, 0:Tn], [bres], qr, scale=0.125)
                              else:
                                  copy_op(DVE, kT[l][:, blk, half, 0:Tn], bk[:, 0:Tn], [bres], ["kT%d_%d_%d" % (l, half, blk)])
                          if which == 1 and want_tm:
                              for tb in range(ntb):
                                  bT, rT = next_bank()
                                  mm_group(bT[:, 0:512], rT, 512, [hT[:, kb, tb * 128:(tb + 1) * 128] for kb in range(KB)],
                                           [sl[:, kb, 0:512] for kb in range(KB)], hres + [sres_])
                                  tm_out(bT, rT, tb, slice(g * 512, (g + 1) * 512), kp_o, ks_o, l)
                  chk('t%d_l%d_s3v' % (ti, l))
                  for nh in range(2):
                      sl, sres_, KB = load_chunk("w_in", l, [(4096 + nh * 512, 512)])
                      for tb in range(ntb):
                          bk, bres = next_bank()
                          mm_group(bk[:, 0:512], bres, 512, [hT[:, kb, tb * 128:(tb + 1) * 128] for kb in range(KB)],
                                   [sl[:, kb, 0:512] for kb in range(KB)], hres + [sres_])
                          veng = evac_eng()
                          copy_op(veng, Vr[l][:, half, tb, nh * 512:(nh + 1) * 512], bk[:, 0:512], [bres],
                                  ["V%d_%d_%d_%d" % (l, half, tb, nh)])
                          if want_tm:
                              tm_out(bk, bres, tb, slice(nh * 512, (nh + 1) * 512), vp_o, vs_o, l, eng=veng)

                  chk('t%d_l%d_s4' % (ti, l))
                  for b in range(NB):
                      ca, cr = cFb(b)
                      bk, bres = next_bank()
                      for g0 in range(0, 31, 4):
                          taps = list(range(g0, min(31, g0 + 4)))
                          rs = []
                          for j in taps:
                              r = dg_i[0] % 8
                              dg_i[0] += 1
                              rs.append(r)
                              P.op(DVE, lambda e, r=r, sc=vcol("conv_dw", l, j * 8 + b): e.tensor_scalar(out=dgr[r][:], in0=ident_b[:], scalar1=sc, scalar2=None, op0=ALU.mult),
                                   reads=["ident_b", "vecs"], writes=["dg%d" % r])

                          def fn(e, taps=taps, rs=rs, bk=bk, b=b, last=last, Tn=Tn):
                              r_ = None
                              for j, r in zip(taps, rs):
                                  if not last:
                                      r_ = e.matmul(bk[:, 0:Tn], dgr[r][:], uT[:, b, j:j + Tn], start=(j == 0), stop=(j == 30))
                                  else:
                                      r_ = e.matmul(bk[:, 0:64], dgr[r][:], uT[:, b, j:j + 64], start=(j == 0), stop=False, skip_group_check=True)
                                      r_ = e.matmul(bk[:, 64:128], dgr[r][:], uT[:, b, 94 + j:94 + j + 64], start=False, stop=(j == 30), skip_group_check=True)
                              return r_
                          P.op(PE, fn, reads=["uT%d" % b] + ["dg%d" % r for r in rs], writes=[bres])
                      P.op(ACT, lambda e, ca=ca, bk=bk, bia=vcol("conv_dw_b", l, b): e.activation(out=ca[:, 0:Tn], in_=bk[:, 0:Tn], func=AF.Identity, bias=bia, scale=1.0),
                           reads=[bres, "vecs"], writes=cr)
                  if not last:
                      P.op(POOL, lambda e, l=l: e.tensor_copy(out=uhist[l][:], in_=uT[:, :, 512:542]),
                           reads=["uT%d" % b for b in range(NB)], writes=["uhist%d" % l])
                  bmu, rmu = next_bank()
                  bms, rms_ = next_bank()
                  for b in range(NB):
                      ca, cr = cFb(b)
                      P.op(PE, lambda e, b=b, ca=ca: e.matmul(bmu[:, 0:Tn], ones_f[:], ca[:, 0:Tn], start=(b == 0), stop=(b == NB - 1)),
                           reads=cr + ["ones_f"], writes=[rmu])
                      j = b % 2
                      P.op(ACT, lambda e, o=sq[j][:, 0:Tn], i_=ca[:, 0:Tn]: e.activation(out=o, in_=i_, func=AF.Square),
                           reads=cr, writes=["sq%d" % j])
                      P.op(PE, lambda e, b=b, j=j: e.matmul(bms[:, 0:Tn], ones_b[:], sq[j][:, 0:Tn], start=(b == 0), stop=(b == NB - 1)),
                           reads=["sq%d" % j, "ones_b"], writes=[rms_])
                  copy_op(ACT, st_mu[:, 0:Tn], bmu[:, 0:Tn], [rmu], ["st_mu"])
                  P.op(DVE, lambda e: e.tensor_tensor(out=st_tmp[:, 0:Tn], in0=st_mu[:, 0:Tn], in1=st_mu[:, 0:Tn], op=ALU.mult),
                       reads=["st_mu"], writes=["st_tmp"])
                  P.op(DVE, lambda e: e.tensor_tensor(out=st_tmp[:, 0:Tn], in0=bms[:, 0:Tn], in1=st_tmp[:, 0:Tn], op=ALU.subtract),
                       reads=[rms_, "st_tmp"], writes=["st_tmp"])
                  P.op(ACT, lambda e: e.activation(out=st_tmp[:, 0:Tn], in_=st_tmp[:, 0:Tn], func=AF.Sqrt, bias=EPS, scale=1.0),
                       reads=["st_tmp"], writes=["st_tmp"])
                  P.op(DVE, lambda e: e.reciprocal(out=st_rstd[:, 0:Tn], in_=st_tmp[:, 0:Tn]), reads=["st_tmp"], writes=["st_rstd"])
                  for b in range(NB):
                      ca, cr = cFb(b)
                      sa, sr = sTb(b)
                      P.op(DVE, lambda e, ca=ca: e.tensor_tensor(out=ca[:, 0:Tn], in0=ca[:, 0:Tn], in1=st_mu[:, 0:Tn], op=ALU.subtract),
                           reads=cr + ["st_mu"], writes=cr)
                      P.op(DVE, lambda e, ca=ca: e.tensor_tensor(out=ca[:, 0:Tn], in0=ca[:, 0:Tn], in1=st_rstd[:, 0:Tn], op=ALU.mult),
                           reads=cr + ["st_rstd"], writes=cr)
                      P.op(ACT, lambda e, ca=ca, sa=sa, b=b, l=l: e.activation(out=sa[:, 0:Tn], in_=ca[:, 0:Tn], func=AF.Silu,
                                                                              bias=vcol("conv_ln_b", l, b), scale=vcol("conv_ln_g", l, b)),
                           reads=cr + ["vecs"], writes=sr)

                  chk('t%d_l%d_s5' % (ti, l))
                  def gates(which):
                      for g in range(2):
                          sl, sres_, KB = load_chunk("w_in", l, [(5120 + which * 1024 + g * 512, 512)])
                          for j in range(4):
                              blk = 4 * g + j
                              bk, bres = next_bank()
                              mm_group(bk[:, 0:Tn], bres, Tn, [sl[:, kb, j * 128:(j + 1) * 128] for kb in range(KB)], hsrc, hres + [sres_])
                              ga, gr = gTb(blk)
                              P.op(ACT, lambda e, ga=ga, bk=bk, blk=blk: e.activation(out=ga[:, 0:Tn], in_=bk[:, 0:Tn], func=AF.Sigmoid,
                                                                                      bias=vcol("b_gate", l, which * 8 + blk), scale=1.0),
                                   reads=[bres, "vecs"], writes=gr)
                  gates(0)
                  ssrc = [sTb(b)[0][:, 0:Tn] for b in range(NB)]
                  ssres = sum([sTb(b)[1] for b in range(NB)], [])
                  for g in range(2):
                      sl, sres_, KB = load_chunk("w_conv_out", l, [(g * 512, 512)])
                      for j in range(4):
                          blk = 4 * g + j
                          bk, bres = next_bank()
                          mm_group(bk[:, 0:Tn], bres, Tn, [sl[:, kb, j * 128:(j + 1) * 128] for kb in range(KB)], ssrc, ssres + [sres_])
                          ga, gr = gTb(blk)
                          ma, mr = mTb(blk)
                          P.op(DVE, lambda e, ma=ma, bk=bk, ga=ga: e.tensor_tensor(out=ma[:, 0:Tn], in0=bk[:, 0:Tn], in1=ga[:, 0:Tn], op=ALU.mult),
                               reads=[bres] + gr, writes=mr)

                  chk('t%d_l%d_s6' % (ti, l))
                  P.dma(SP, lambda e, l=l: [e.dma_start(out=bnear[:], in_=bns[l])], "bnl", reads=["bns%d" % l], writes=["bnear"])

                  def att_all(qc0, qc1, kblocks, LA=2):
                      units = [(i, kb_, s) for i in range(8) for kb_ in kblocks for s in range(2)]
                      nper = 2 * len(kblocks)
                      info = {}

                      def emit_score(u):
                          i, kb_, s = units[u]
                          r0, r1 = kb_["rows"]
                          c0, c1 = kb_["cols"]
                          qa, qr = qTb(i)
                          h = 2 * i + s
                          j = u % 4
                          j2 = u % NPT
                          Sb, Sres = banks[j], "bk%d" % j
                          kap = kb_["kTf"](s, i)
                          qap = qa[64 * s:64 * s + 64, c0:c1]
                          tb_ = bnear[:, h * 256:(h + 1) * 256]

                          def fs(e, Sb=Sb, kap=kap, qap=qap, kb_=kb_, tb_=tb_, r0=r0, r1=r1, c0=c0, c1=c1):
                              r = e.matmul(Sb[r0:r1, c0:c1], kap, qap, start=True, stop=False, skip_group_check=True)
                              for (tc0, ncol, oc0) in kb_["near"]:
                                  r = e.matmul(Sb[r0:r1, oc0:oc0 + ncol], ident_b[r0:r1, r0:r1], tb_[r0:r1, tc0:tc0 + ncol],
                                               start=False, stop=False, skip_group_check=True)
                              if kb_["far"] is not None:
                                  oc0 = kb_["far"]
                                  r = e.matmul(Sb[r0:r1, oc0:oc0 + 64], ident_b[r0:r1, r0:r1], farm[r0:r1, :],
                                               start=False, stop=False, skip_group_check=True)
                              return r
                          P.op(PE, fs, reads=kb_["kres"](i) + qr + ["ident_b", "bnear", "farm"], writes=[Sres])
                          cf = vecs[r0:r1, 2 * VPL + 8 + l * NH + h:2 * VPL + 8 + l * NH + h + 1]
                          P.op(ACT, lambda e, o=PT[j2][r0:r1, c0:c1], i_=Sb[r0:r1, c0:c1], cf=cf: e.activation(out=o, in_=i_, func=AF.Exp, bias=cf, scale=1.0),
                               reads=[Sres, "vecs"], writes=["PT%d" % j2])

                      def emit_pv(u):
                          i, kb_, s = units[u]
                          r0, r1 = kb_["rows"]
                          c0, c1 = kb_["cols"]
                          h = 2 * i + s
                          j2 = u % NPT
                          Ob, Ores = banks[4 + i % 2], "bk%d" % (4 + i % 2)
                          Db, Dres = banks[6 + i % 2], "bk%d" % (6 + i % 2)
                          fst = (u % nper) < 2
                          vap = kb_["V"][r0:r1, h * 64:(h + 1) * 64]

                          def fpv(e, vap=vap, j=j2, r0=r0, r1=r1, c0=c0, c1=c1, s=s, fst=fst, Ob=Ob, Db=Db):
                              e.matmul(Ob[64 * s:64 * s + 64, c0:c1], vap, PT[j][r0:r1, c0:c1], start=fst, stop=False, skip_group_check=True)
                              return e.matmul(Db[64 * s:64 * s + 64, c0:c1], one1_b[r0:r1, :], PT[j][r0:r1, c0:c1], start=fst, stop=False,
                                              skip_group_check=True)
                          P.op(PE, fpv, reads=["PT%d" % j2, "one1_b"] + kb_["vres"], writes=[Ores, Dres])
                          if (u % nper) == nper - 1:
                              aa, ar = aTb(i)
                              P.op(DVE, lambda e, Db=Db: e.reciprocal(out=st_tmp[:, qc0:qc1], in_=Db[:, qc0:qc1]), reads=[Dres], writes=["st_tmp"])
                              P.op(DVE, lambda e, aa=aa, Ob=Ob: e.tensor_tensor(out=aa[:, qc0:qc1], in0=Ob[:, qc0:qc1], in1=st_tmp[:, qc0:qc1], op=ALU.mult),
                                   reads=[Ores, "st_tmp"], writes=ar)

                      for idx in range(len(units) + LA):
                          if idx - LA >= 0:
                              emit_pv(idx - LA)
                          if idx < len(units):
                              emit_score(idx)

                  def kblock_pair(hf, pb, rows, cols, near, far):
                      return dict(rows=rows, cols=cols, near=near, far=far, kT=None,
                                  kTf=lambda s, i, hf=hf, pb=pb, rows=rows: kT[l][64 * s:64 * s + 64, i, hf, pb * 128 + rows[0]:pb * 128 + rows[1]],
                                  kres=lambda i, hf=hf: ["kT%d_%d_%d" % (l, hf, i)],
                                  V=Vr[l][:, hf, pb, :], vres=["V%d_%d_%d_%d" % (l, hf, pb, nh) for nh in range(2)])

                  if not last:
                      kbl = []
                      for b in range(8):
                          if b < 4 and ti == 0:
                              continue
                          ilo = max(0, 2 * b - 8)
                          ihi = min(7, 2 * b + 1)
                          near = []
                          cs_ = [c for c in range(4) if 0 <= 2 * b - 8 + c <= 7]
                          if b >= 3 and cs_:
                              near.append((cs_[0] * 64, len(cs_) * 64, (2 * b - 8 + cs_[0]) * 64))
                          far = (2 * b + 1) * 64 if b <= 3 else None
                          hf = (1 - half) if b < 4 else half
                          kbl.append(kblock_pair(hf, b % 4, (0, 128), (ilo * 64, (ihi + 1) * 64), near, far))
                      kbl.sort(key=lambda d: -(d["cols"][1] - d["cols"][0]))
                      att_all(0, Tn, kbl)
                  else:
                      kbl = []
                      for b in range(4):
                          near = [(128, 64, 0)] if b == 3 else []
                          kbl.append(kblock_pair(1 - half, b, (0, 128), (0, 64), near, None))
                      kbl.append(kblock_pair(half, 0, (0, 128), (0, 64), [(0, 64, 0)], None))
                      att_all(0, 64, kbl)
                      chk('t%d_l%d_s6b' % (ti, l))
                      oh = 1 - half
                      for tb in range(4):
                          stg, sres = next_xs()
                          P.dma(POOL, lambda e, o=stg[:], s=ck_d[l, tb * 128:(tb + 1) * 128, :]: [e.dma_start(out=o, in_=s)], sres, writes=[sres])
                          for hh in range(2):
                              bk, bres = next_bank()

                              def fn(e, stg=stg, bk=bk, hh=hh):
                                  r = None
                                  for f in range(4):
                                      fb = hh * 4 + f
                                      r = e.transpose(bk[:, f * 128:(f + 1) * 128], stg[:, fb * 128:(fb + 1) * 128], ident_f[:])
                                  return r
                              P.op(PE, fn, reads=[sres, "ident_f"], writes=[bres])
                              copy_op(evac_eng(), kT[l][:, hh * 4:hh * 4 + 4, oh, tb * 128:(tb + 1) * 128],
                                      bk[:, 0:512].rearrange("p (f c) -> p f c", c=128), [bres],
                                      ["kT%d_%d_%d" % (l, oh, hh * 4 + f) for f in range(4)])
                      P.dma(POOL, lambda e, l=l, oh=oh: [e.dma_start(out=Vr[l][:, oh, :, :], in_=cv_d[l].rearrange("(b p) n -> p b n", p=128))],
                            "cvl", writes=["V%d_%d_%d_%d" % (l, oh, tb, nh) for tb in range(4) for nh in range(2)])
                      chk('t%d_l%d_s6c' % (ti, l))
                      P.dma(POOL, lambda e, l=l: [e.dma_start(out=ks_o[l, 0:448, :], in_=ck_d[l, 64:512, :]),
                                                 e.dma_start(out=vs_o[l, 0:448, :], in_=cv_d[l, 64:512, :])], "occ", n=2, is_output=True)
                      chk('t%d_l%d_s6d' % (ti, l))
                      kbl = []
                      for b in range(4):
                          near = [(128, 64, 64)] if b == 3 else []
                          kbl.append(kblock_pair(oh, b, (0, 128), (64, 128), near, None))
                      kbl.append(kblock_pair(half, 0, (0, 128), (64, 128), [(64, 64, 64)], 64))
                      att_all(64, 128, kbl)

                  chk('t%d_l%d_s7' % (ti, l))
                  gates(1)
                  asrc = [aTb(b)[0][:, 0:Tn] for b in range(NB)]
                  asres = sum([aTb(b)[1] for b in range(NB)], [])
                  for g in range(2):
                      sl, sres_, KB = load_chunk("w_att_out", l, [(g * 512, 512)])
                      for j in range(4):
                          blk = 4 * g + j
                          bk, bres = next_bank()
                          mm_group(bk[:, 0:Tn], bres, Tn, [sl[:, kb, j * 128:(j + 1) * 128] for kb in range(KB)], asrc, asres + [sres_])
                          ga, gr = gTb(blk)
                          ma, mr = mTb(blk)
                          tj = blk % 2
                          P.op(DVE, lambda e, bk=bk, ga=ga, tj=tj: e.tensor_tensor(out=tmpb[tj][:, 0:Tn], in0=bk[:, 0:Tn], in1=ga[:, 0:Tn], op=ALU.mult),
                               reads=[bres] + gr, writes=["tmpb%d" % tj])
                          P.op(POOL, lambda e, ma=ma, tj=tj: e.tensor_tensor(out=ma[:, 0:Tn], in0=ma[:, 0:Tn], in1=tmpb[tj][:, 0:Tn], op=ALU.add),
                               reads=["tmpb%d" % tj] + mr, writes=mr)

                  msrc = [mTb(b)[0][:, 0:Tn] for b in range(NB)]
                  msres = sum([mTb(b)[1] for b in range(NB)], [])
                  for g in range(2):
                      sl, sres_, KB = load_chunk("w_out", l, [(g * 512, 512)])
                      for j in range(4):
                          blk = 4 * g + j
                          bk, bres = next_bank()
                          mm_group(bk[:, 0:Tn], bres, Tn, [sl[:, kb, j * 128:(j + 1) * 128] for kb in range(KB)], msrc, msres + [sres_])
                          P.op(DVE, lambda e, bk=bk, blk=blk: e.tensor_tensor(out=xT[:, blk, 0:Tn], in0=bk[:, 0:Tn], in1=xT[:, blk, 0:Tn], op=ALU.add),
                               reads=[bres, "xT%d" % blk], writes=["xT%d" % blk])

                  chk('t%d_l%d_s9' % (ti, l))
                  rmsnorm_to_hT(Tn, "norm_ffn", l)
                  hsrc, hres = hT_src(Tn)
                  if last:
                      bkS, bresS = next_bank()
                      for pi, (c0, cw) in enumerate([(0, 1024), (1024, 1024), (2048, 768)]):
                          stg, sres = next_xs()
                          P.dma(POOL, lambda e, o=stg[0:2, 0:cw], s=sffn_d[l, :, c0:c0 + cw]: [e.dma_start(out=o, in_=s)], sres, writes=[sres])

                          def fn(e, stg=stg, bkS=bkS, pi=pi, cw=cw):
                              r = None
                              for q in range(cw // 128):
                                  fb = pi * 8 + q
                                  r = e.transpose(bkS[:, fb * 2:(fb + 1) * 2], stg[0:2, q * 128:(q + 1) * 128], ident_f[0:2, 0:2])
                              return r
                          P.op(PE, fn, reads=[sres, "ident_f"], writes=[bresS])
                      copy_op(ACT, sfT[:], bkS[:, 0:2 * FB].rearrange("p (f c) -> p f c", c=2), [bresS], ["sfT"])
                  for g in range(11):
                      sl, sres_, KB = load_chunk("w_ffn_up", l, [(g * 256, 256), (DFF + g * 256, 256)])
                      for j in range(2):
                          fb = 2 * g + j
                          fj = fb % 2
                          ur = ["upr%d" % fj]
                          ua = upr[fj]
                          if ti == 0:
                              P.op(POOL, lambda e, ua=ua: e.memset(ua[:, 0:2], 0.0), writes=ur)
                          else:
                              P.op(POOL, lambda e, ua=ua, fb=fb, l=l: e.tensor_copy(out=ua[:, 0:2], in_=uphist[l][:, fb, :]),
                                   reads=["uph%d_%d" % (l, fb)], writes=ur)
                          if last:
                              P.op(POOL, lambda e, ua=ua, fb=fb: e.tensor_copy(out=ua[:, 66:68], in_=sfT[:, fb, :]), reads=["sfT"], writes=ur)
                          bU, rU = next_bank()
                          mm_group(bU[:, 0:Tn], rU, Tn, [sl[:, kb, j * 128:(j + 1) * 128] for kb in range(KB)], hsrc, hres + [sres_])
                          bG, rG = next_bank()
                          mm_group(bG[:, 0:Tn], rG, Tn, [sl[:, kb, 256 + j * 128:256 + (j + 1) * 128] for kb in range(KB)], hsrc, hres + [sres_])
                          fa, fr = fTb(fb)
                          accv = pseg(facc[fj])
                          copy_op(ACT, seg(ua[:, 2:514], Tn, 66), pseg(bU), [rU], ur)
                          P.op(ACT, lambda e, accv=accv, src=pseg(bU), bia=vcol("ffn_dw_b", l, fb), sc=vcol("ffn_dw", l, 2 * FB + fb):
                               e.activation(out=accv, in_=src, func=AF.Identity, bias=bia, scale=sc),
                               reads=[rU, "vecs"], writes=["facc%d" % fj])
                          for tap in (1, 0):
                              P.op(DVE, lambda e, accv=accv, src=seg(ua[:, tap:514], Tn, 66), sc=vcol("ffn_dw", l, tap * FB + fb): e.scalar_tensor_tensor(
                                  out=accv, in0=src, scalar=sc, in1=accv, op0=ALU.mult, op1=ALU.add),
                                  reads=ur + ["vecs", "facc%d" % fj], writes=["facc%d" % fj])
                          if not last:
                              P.op(POOL, lambda e, ua=ua, fb=fb, l=l: e.tensor_copy(out=uphist[l][:, fb, :], in_=ua[:, 512:514]),
                                   reads=ur, writes=["uph%d_%d" % (l, fb)])
                          P.op(ACT, lambda e, o=sg[fj][:, 0:Tn], i_=facc[fj][:, 0:Tn]: e.activation(out=o, in_=i_, func=AF.Gelu),
                               reads=["facc%d" % fj], writes=["sg%d" % fj])
                          P.op(DVE, lambda e, o=fa[:, 0:Tn], a=bG[:, 0:Tn], b_=sg[fj][:, 0:Tn]: e.tensor_tensor(out=o, in0=a, in1=b_, op=ALU.mult),
                               reads=[rG, "sg%d" % fj], writes=fr)
                      if last:
                          bT, rT = next_bank()
                          mm_group(bT[:, 0:256], rT, 256, [hT[:, kb, 0:128] for kb in range(KB)], [sl[:, kb, 0:256] for kb in range(KB)],
                                   hres + [sres_])
                          sm = g % 2
                          copy_op(evac_eng(), smst[:, sm, :], bT[:, 0:256], [rT], ["smst%d" % sm])
                          P.dma(POOL, lambda e, g=g, l=l, sm=sm: [e.dma_start(out=ffnp_o[l, :, g * 256:(g + 1) * 256], in_=smst[62:64, sm, :]),
                                                                 e.dma_start(out=ffns_o[l, :, g * 256:(g + 1) * 256], in_=smst[126:128, sm, :])],
                                "o_smst%d" % sm, reads=["smst%d" % sm], n=2, is_output=True)

                  chk('t%d_l%d_s11' % (ti, l))
                  fsrc = [fTb(fb)[0][:, 0:Tn] for fb in range(FB)]
                  fsres = sorted(set(sum([fTb(fb)[1] for fb in range(FB)], [])))
                  for blk in range(8):
                      sl, sres_, KB = load_chunk("w_ffn_down", l, [(blk * 128, 128)])
                      bk, bres = next_bank()
                      mm_group(bk[:, 0:Tn], bres, Tn, [sl[:, kb, 0:128] for kb in range(KB)], fsrc, fsres + [sres_])
                      P.op(DVE, lambda e, bk=bk, blk=blk: e.tensor_tensor(out=xT[:, blk, 0:Tn], in0=bk[:, 0:Tn], in1=xT[:, blk, 0:Tn], op=ALU.add),
                           reads=[bres, "xT%d" % blk], writes=["xT%d" % blk])

                  chk('t%d_l%d_s12' % (ti, l))
                  rmsnorm_to_hT(Tn, "norm_ple", l)
                  hsrc, hres = hT_src(Tn)
                  p0_ensure("w_ple_proj", l, [0])
                  P.dma(SP, lambda e, l=l: [e.dma_start(out=pslot[:], in_=ws["w_ple_proj"][l].rearrange("(k p) n -> p k n", p=128))],
                        "wlp", reads=wres("w_ple_proj", l, [0]), writes=["pslot"])
                  for g in range(2):
                      sl, sres_, KB = load_chunk("w_ple_gate", l, [(g * 512, 512)])
                      for j in range(4):
                          blk = 4 * g + j
                          bG, rG = next_bank()
                          mm_group(bG[:, 0:Tn], rG, Tn, [sl[:, kb, j * 128:(j + 1) * 128] for kb in range(KB)], hsrc, hres + [sres_])
                          bP, rP = next_bank()
                          mm_group(bP[:, 0:Tn], rP, Tn, [pslot[:, kb, blk * 128:(blk + 1) * 128] for kb in range(2)],
                                   [pT[:, kb, 0:Tn] for kb in range(2)], ["pslot", "pT"])
                          sj = blk % 2
                          P.op(ACT, lambda e, sj=sj, bG=bG: e.activation(out=sg[sj][:, 0:Tn], in_=bG[:, 0:Tn], func=AF.Sigmoid),
                               reads=[rG], writes=["sg%d" % sj])
                          P.op(DVE, lambda e, sj=sj, bP=bP: e.tensor_tensor(out=facc[sj][:, 0:Tn], in0=bP[:, 0:Tn], in1=sg[sj][:, 0:Tn], op=ALU.mult),
                               reads=[rP, "sg%d" % sj], writes=["facc%d" % sj])
                          P.op(DVE, lambda e, sj=sj, blk=blk: e.tensor_tensor(out=xT[:, blk, 0:Tn], in0=facc[sj][:, 0:Tn], in1=xT[:, blk, 0:Tn], op=ALU.add),
                               reads=["facc%d" % sj, "xT%d" % blk], writes=["xT%d" % blk])

              chk('t%d_fin' % ti)
              p0_advance(100000)
              rms_stats(Tn, lambda b: (xT[:, b, 0:Tn], ["xT%d" % b]))
              for b in range(NB):
                  ca, cr = cFb(b)
                  P.op(DVE, lambda e, b=b, ca=ca: e.scalar_tensor_tensor(out=ca[:, 0:Tn], in0=xT[:, b, 0:Tn], scalar=vecs[:, 2 * VPL + b:2 * VPL + b + 1],
                                                                       in1=st_rstd[:, 0:Tn], op0=ALU.mult, op1=ALU.mult),
                       reads=["xT%d" % b, "st_rstd", "vecs"], writes=cr)
              for tb in range(ntb):
                  stg, sres = next_xs()
                  for hh in range(2):
                      bk, bres = next_bank()

                      def fn(e, bk=bk, hh=hh, tb=tb):
                          r = None
                          for f in range(4):
                              ca, _ = cFb(hh * 4 + f)
                              r = e.transpose(bk[:, f * 128:(f + 1) * 128], ca[:, tb * 128:(tb + 1) * 128], ident_f[:])
                          return r
                      P.op(PE, fn, reads=sum([cFb(hh * 4 + f)[1] for f in range(4)], []) + ["ident_f"], writes=[bres])
                      copy_op(evac_eng(), stg[:, hh * 512:(hh + 1) * 512], bk[:, 0:512], [bres], [sres])
                  P.dma(POOL, lambda e, stg=stg, r0=t0 + tb * 128: [e.dma_start(out=y_o[r0:r0 + 128, :], in_=stg[:])],
                        "o_" + sres, reads=[sres], is_output=True)

        except _Stop:
            pass
        P.finish(POOL)
        P.emit()
    return nc


_NC = None


def _lay(v):
    v = np.asarray(v, np.float32)
    lead = v.shape[:-1]
    nb = v.shape[-1] // 128
    v = v.reshape(lead + (nb, 128))
    v = np.moveaxis(v, -1, 0)
    return v.reshape(128, -1)


def kernel(**inp):
    global _NC
    f = lambda k: np.ascontiguousarray(np.asarray(inp[k], np.float32))
    x_prompt, x_sample, p_prompt, p_sample = f("x_prompt"), f("x_sample"), f("p_prompt"), f("p_sample")
    ck, cv, sconv, sffn = f("cache_att_k"), f("cache_att_v"), f("state_conv"), f("state_ffn_conv")
    rel = f("rel_table")

    cols = []
    for l in range(2):
        cols += [_lay(f("norm_mix")[l]), _lay(f("conv_dw")[l]), _lay(f("conv_dw_b")[l]), _lay(f("conv_ln_g")[l]),
                 _lay(f("conv_ln_b")[l]), _lay(f("b_gate")[l]), _lay(f("norm_ffn")[l]), _lay(f("ffn_dw")[l]),
                 _lay(f("ffn_dw_b")[l]), _lay(f("norm_ple")[l])]
    cols.append(_lay(f("norm_final")))
    cfar = np.broadcast_to(rel[:, :, 256].reshape(1, 32), (128, 32))
    cols.append(cfar)
    vecs = np.ascontiguousarray(np.concatenate(cols, axis=1), np.float32)
    assert vecs.shape == (128, NV), vecs.shape

    kl = np.arange(64)[:, None]
    ql = np.arange(64)[None, :]
    bn = np.zeros((2, NH, 128, 256), np.float32)
    for o in range(4):
        idx = np.clip(64 * o + ql - kl, -128, 128) + 128
        Bo = rel[:, :, idx]
        bn[:, :, 0:64, o * 64:(o + 1) * 64] = Bo
        if o < 3:
            bn[:, :, 64:128, (o + 1) * 64:(o + 2) * 64] = Bo
    bn[:, :, 64:128, 0:64] = MASKV
    bn_raw = bn
    bnear = np.ascontiguousarray(np.moveaxis(bn_raw, 2, 0).reshape(128, 2 * NH * 256))
    ident = np.eye(128, dtype=np.float32)

    in_maps = []
    for c in range(8):
        s, hb = c // 2, c % 2
        ch0 = 0 if hb == 0 else 64 - NCHK
        xs_ = np.concatenate([x_prompt[s, ch0 * 64:(ch0 + NCHK) * 64], x_sample[c]], axis=0)
        ps_ = np.concatenate([p_prompt[:, s, ch0 * 64:(ch0 + NCHK) * 64], p_sample[:, c]], axis=1)
        m = {"x": np.ascontiguousarray(xs_), "p": np.ascontiguousarray(ps_),
             "ck": np.ascontiguousarray(ck[:, c].reshape(2, 512, D)), "cv": np.ascontiguousarray(cv[:, c].reshape(2, 512, D)),
             "sconv": np.ascontiguousarray(sconv[:, c]), "sffn": np.ascontiguousarray(sffn[:, c]),
             "vecs": vecs, "bnear": bnear, "ident": ident}
        for (n, K, N) in WSPEC:
            m[n] = f(n)
        in_maps.append(m)

    if _NC is None:
        _NC = build_nc()
    res = run_bass_kernel_spmd(_NC, in_maps, core_ids=list(range(8)))
    R = res.results

    y_prompt = np.zeros((4, 4096, D), np.float32)
    y_sample = np.zeros((8, 64, D), np.float32)
    nkp = np.zeros((2, 4, 512, NH, 64), np.float32)
    nvp = np.zeros_like(nkp)
    ncp = np.zeros((2, 4, 30, D), np.float32)
    nfp = np.zeros((2, 4, 2, DFF), np.float32)
    nks = np.zeros((2, 8, 512, NH, 64), np.float32)
    nvs = np.zeros_like(nks)
    ncs = np.zeros((2, 8, 30, D), np.float32)
    nfs = np.zeros((2, 8, 2, DFF), np.float32)
    for c in range(8):
        s, hb = c // 2, c % 2
        r = R[c]
        if hb == 0:
            y_prompt[s, 0:NCHK * 64] = r["y"][0:NCHK * 64]
        else:
            y_prompt[s, NCHK * 64:] = r["y"][HALO * 64:NCHK * 64]
            nkp[:, s] = r["kp"].reshape(2, 512, NH, 64)
            nvp[:, s] = r["vp"].reshape(2, 512, NH, 64)
            ncp[:, s] = r["convp"]
            nfp[:, s] = r["ffnp"]
        y_sample[c] = r["y"][NCHK * 64:]
        nks[:, c] = r["ks"].reshape(2, 512, NH, 64)
        nvs[:, c] = r["vs"].reshape(2, 512, NH, 64)
        ncs[:, c] = r["convs"]
        nfs[:, c] = r["ffns"]
    return (y_prompt, y_sample, nkp, nvp, ncp, nfp, nks, nvs, ncs, nfs)
```

```python
import contextlib
import os
import types
import numpy as np
import concourse.bass as bass
import concourse.mybir as mybir
from concourse.bass_utils import run_bass_kernel_spmd

F32 = mybir.dt.float32
BF16 = mybir.dt.bfloat16
AF = mybir.ActivationFunctionType
ALU = mybir.AluOpType
PE, ACT, DVE, POOL, SP = "pe", "act", "dve", "pool", "sp"

D = 1024
NB = 8
NH = 16
DFF = 2816
FB = 22
NIN = 7168
NCHK = 41
HALO = 18
NTOK = NCHK * 64 + 64
TILES = [(t * 512, 512) for t in range(5)] + [(2560, 128)]
EPS = 1e-6
MASKV = -30000.0

VOFF = {}
_c = 0
for _n, _w in [("norm_mix", 8), ("conv_dw", 31 * 8), ("conv_dw_b", 8), ("conv_ln_g", 8), ("conv_ln_b", 8),
               ("b_gate", 16), ("norm_ffn", 8), ("ffn_dw", 3 * 22), ("ffn_dw_b", 22), ("norm_ple", 8)]:
    VOFF[_n] = _c
    _c += _w
VPL = _c
NV = 2 * VPL + 8 + 32

WSPEC = [("w_in", 1024, NIN), ("w_conv_out", 1024, 1024), ("w_att_out", 1024, 1024), ("w_out", 1024, 1024),
         ("w_ffn_up", 1024, 2 * DFF), ("w_ffn_down", DFF, 1024), ("w_ple_gate", 1024, 1024),
         ("w_ple_proj", 256, 1024)]


def _freeze(fn, depth=0):
    if not isinstance(fn, types.FunctionType) or fn.__closure__ is None or depth > 4:
        return fn
    cells = []
    for c in fn.__closure__:
        try:
            v = c.cell_contents
        except ValueError:
            cells.append(c)
            continue
        if isinstance(v, types.FunctionType) and v.__closure__ is not None:
            v = _freeze(v, depth + 1)
        cells.append(types.CellType(v))
    g = types.FunctionType(fn.__code__, fn.__globals__, fn.__name__, fn.__defaults__, tuple(cells))
    g.__kwdefaults__ = fn.__kwdefaults__
    return g


class _Stop(Exception):
    pass


KSTOP = os.environ.get("KSTOP", "")


def chk(tag):
    if KSTOP and tag == KSTOP:
        raise _Stop()


class Prog:
    def __init__(self, nc, stack):
        self.nc = nc
        self.stack = stack
        self.engs = [PE, ACT, DVE, POOL, SP]
        self.ops = {e: [] for e in self.engs}
        self.semobj = {}
        for e in self.engs:
            self.semobj["es_" + e] = stack.enter_context(nc.semaphore("es_" + e))
        self.ecount = {e: 0 for e in self.engs}
        self.waited = {e: {} for e in self.engs}
        self.res = {}
        self.dcount = {}
        self.out_tokens = []

    def _deps(self, eng, reads, writes):
        deps = set()
        for r in reads:
            st = self.res.get(r)
            if st and st["w"]:
                deps.add(st["w"])
        for w in writes:
            st = self.res.get(w)
            if st:
                if st["w"]:
                    deps.add(st["w"])
                deps |= st["r"]
        waits = []
        wd = self.waited[eng]
        own = "es_" + eng
        best = {}
        for (sname, val) in deps:
            best[sname] = max(best.get(sname, 0), val)
        for (sname, val) in sorted(best.items()):
            if eng == PE and sname == own:
                continue
            if wd.get(sname, 0) >= val:
                continue
            wd[sname] = val
            waits.append((sname, val))
        return waits

    def _commit(self, tok, reads, writes):
        for r in reads:
            st = self.res.setdefault(r, {"w": None, "r": set()})
            st["r"].add(tok)
        for w in writes:
            st = self.res.setdefault(w, {"w": None, "r": set()})
            st["w"] = tok
            st["r"] = set()

    def op(self, eng, fn, reads=(), writes=()):
        waits = self._deps(eng, reads, writes)
        self.ecount[eng] += 1
        tok = ("es_" + eng, self.ecount[eng])
        self.ops[eng].append((_freeze(fn), waits, ("es_" + eng, 1)))
        self._commit(tok, reads, writes)
        return tok

    def dma(self, eng, fn, sem, reads=(), writes=(), n=1, is_output=False):
        key = "ds_" + sem
        if key not in self.semobj:
            self.semobj[key] = self.stack.enter_context(self.nc.semaphore(key))
            self.dcount[key] = 0
        waits = self._deps(eng, reads, writes)
        self.dcount[key] += 16 * n
        tok = (key, self.dcount[key])
        self.ops[eng].append((_freeze(fn), waits, (key, 16)))
        self._commit(tok, reads, writes)
        if is_output:
            self.out_tokens.append(tok)
        return tok

    def finish(self, eng=SP):
        last = {}
        for (s, v) in self.out_tokens:
            last[s] = max(last.get(s, 0), v)
        self.ops[eng].append((None, sorted(last.items()), None))

    def emit(self):
        nc = self.nc
        with nc.Block() as block:
            def run(e, name):
                for (fn, waits, inc) in self.ops[name]:
                    for (s, v) in waits:
                        e.wait_ge(self.semobj[s], v)
                    if fn is None:
                        continue
                    r = fn(e)
                    if isinstance(r, (list, tuple)):
                        for ins in r:
                            ins.then_inc(self.semobj[inc[0]], inc[1])
                    else:
                        r.then_inc(self.semobj[inc[0]], inc[1])

            @block.sync
            def _(e):
                run(e, SP)

            @block.tensor
            def _(e):
                run(e, PE)

            @block.scalar
            def _(e):
                run(e, ACT)

            @block.vector
            def _(e):
                run(e, DVE)

            @block.gpsimd
            def _(e):
                run(e, POOL)


def au(off, nbytes):
    return ["ar%d" % u for u in range(off // 2048, (off + nbytes - 1) // 2048 + 1)]


def build_nc():
    nc = bass.Bass("TRN2", target_bir_lowering=False)

    def din(name, shape, dt=F32):
        return nc.dram_tensor(name, list(shape), dt, kind="ExternalInput").ap()

    def dout(name, shape, dt=F32):
        return nc.dram_tensor(name, list(shape), dt, kind="ExternalOutput").ap()

    x_d = din("x", [NTOK, D])
    p_d = din("p", [2, NTOK, 256])
    ck_d = din("ck", [2, 512, D])
    cv_d = din("cv", [2, 512, D])
    sconv_d = din("sconv", [2, 30, D])
    sffn_d = din("sffn", [2, 2, DFF])
    w_d = {n: din(n, [2, K, N]) for (n, K, N) in WSPEC}
    vecs_d = din("vecs", [128, NV])
    bnear_d = din("bnear", [128, 2 * NH * 256])
    ident_d = din("ident", [128, 128])

    y_o = dout("y", [NTOK, D])
    kp_o = dout("kp", [2, 512, D])
    vp_o = dout("vp", [2, 512, D])
    ks_o = dout("ks", [2, 512, D])
    vs_o = dout("vs", [2, 512, D])
    convp_o = dout("convp", [2, 30, D])
    convs_o = dout("convs", [2, 30, D])
    ffnp_o = dout("ffnp", [2, 2, DFF])
    ffns_o = dout("ffns", [2, 2, DFF])

    ws = {n: nc.dram_tensor("ws_" + n, [2, K, N], BF16, kind="Internal").ap() for (n, K, N) in WSPEC}

    bns = nc.dram_tensor("bns", [2, 128, NH * 256], BF16, kind="Internal").ap()

    with contextlib.ExitStack() as st:
        P = Prog(nc, st)

        def sb(name, shape, dt):
            return st.enter_context(nc.sbuf_tensor(name, list(shape), dt))

        xT = sb("xT", [128, NB, 512], F32)
        hT = sb("hT", [128, NB, 512], BF16)
        uT = sb("uT", [128, NB, 542], BF16)
        kT = [sb("kT%d" % l, [128, NB, 2, 512], BF16) for l in range(2)]
        Vr = [sb("Vr%d" % l, [128, 2, 4, D], BF16) for l in range(2)]
        uhist = [sb("uhist%d" % l, [128, NB, 30], BF16) for l in range(2)]
        uphist = [sb("uphist%d" % l, [128, FB, 2], BF16) for l in range(2)]
        NPT = 3
        PT = [sb("PT%d" % j, [128, 512], BF16) for j in range(NPT)]
        NSLOT = 2
        slots = [sb("slot%d" % j, [128, 4096], BF16) for j in range(NSLOT)]
        pslot = sb("pslot", [128, 2, D], BF16)
        vecs = sb("vecs_sb", [128, NV], F32)
        bnear = sb("bnear_sb", [128, NH * 256], BF16)
        ident_f = sb("ident_f", [128, 128], F32)
        ident_b = sb("ident_b", [128, 128], BF16)
        ones_b = sb("ones_b", [128, 128], BF16)
        ones_f = sb("ones_f", [128, 128], F32)
        one1_b = sb("one1_b", [128, 64], BF16)
        farm = sb("farm", [128, 64], BF16)
        pT = sb("pT", [128, 2, 512], BF16)
        xs = [sb("xs%d" % j, [128, D], F32) for j in range(2)]
        upr = [sb("upr%d" % j, [128, 514], BF16) for j in range(2)]
        sfT = sb("sfT", [128, FB, 2], BF16)
        dgr = [sb("dgr%d" % j, [128, 128], BF16) for j in range(8)]
        dg_i = [0]
        st_mu = sb("st_mu", [128, 512], F32)
        st_tmp = sb("st_tmp", [128, 512], F32)
        st_rstd = sb("st_rstd", [128, 512], F32)
        sq = [sb("sq%d" % j, [128, 512], BF16) for j in range(2)]
        sg = [sb("sg%d" % j, [128, 512], F32) for j in range(2)]
        facc = [sb("facc%d" % j, [128, 512], F32) for j in range(2)]
        tmpb = [sb("tmpb%d" % j, [128, 512], BF16) for j in range(2)]
        smst = sb("smst", [128, 2, 256], F32)
        arena = sb("arena", [128, 20480], BF16)
        banks = [st.enter_context(nc.psum_tensor("bk%d" % j, [128, 512], F32)) for j in range(8)]

        cF = arena[:, 0:8192].bitcast(F32)

        def cFb(b):
            return cF[:, b * 512:(b + 1) * 512], au(b * 2048, 2048)

        def gTb(b):
            return arena[:, b * 512:(b + 1) * 512], au(b * 1024, 1024)

        def mTb(b):
            return arena[:, 4096 + b * 512:4096 + (b + 1) * 512], au(8192 + b * 1024, 1024)

        def sTb(b):
            return arena[:, 8192 + b * 512:8192 + (b + 1) * 512], au(16384 + b * 1024, 1024)

        def qTb(b):
            return arena[:, 12288 + b * 512:12288 + (b + 1) * 512], au(24576 + b * 1024, 1024)

        def aTb(b):
            return arena[:, 16384 + b * 512:16384 + (b + 1) * 512], au(32768 + b * 1024, 1024)

        def fTb(fb):
            return arena[:, fb * 512:(fb + 1) * 512], au(fb * 1024, 1024)

        bank_i = [0]

        def next_bank():
            j = bank_i[0] % 8
            bank_i[0] += 1
            return banks[j], "bk%d" % j

        xs_i = [0]

        def next_xs():
            j = xs_i[0] % 2
            xs_i[0] += 1
            return xs[j], "xs%d" % j

        evac_i = [0]

        def evac_eng():
            evac_i[0] += 1
            return ACT if evac_i[0] % 2 else DVE

        def vcol(name, l, idx):
            c = l * VPL + VOFF[name] + idx
            return vecs[:, c:c + 1]

        def copy_op(eng, out, in_, reads, writes, scale=None):
            if eng == ACT:
                if scale is None:
                    P.op(ACT, lambda e: e.activation(out=out, in_=in_, func=AF.Copy), reads=reads, writes=writes)
                else:
                    P.op(ACT, lambda e: e.activation(out=out, in_=in_, func=AF.Copy, scale=scale), reads=reads, writes=writes)
            else:
                if scale is None:
                    P.op(eng, lambda e: e.tensor_copy(out=out, in_=in_), reads=reads, writes=writes)
                else:
                    P.op(eng, lambda e: e.tensor_scalar(out=out, in0=in_, scalar1=scale, scalar2=None, op0=ALU.mult),
                         reads=reads, writes=writes)

        P.dma(POOL, lambda e: [e.dma_start(out=vecs[:], in_=vecs_d)], "c0", writes=["vecs"])
        P.dma(POOL, lambda e: [e.dma_start(out=ident_f[:], in_=ident_d)], "c1", writes=["ident_f"])
        P.op(DVE, lambda e: e.tensor_copy(out=ident_b[:], in_=ident_f[:]), reads=["ident_f"], writes=["ident_b"])
        P.op(POOL, lambda e: e.memset(ones_b[:], 1.0 / 1024.0), writes=["ones_b"])
        P.op(POOL, lambda e: e.memset(ones_f[:], 1.0 / 1024.0), writes=["ones_f"])
        P.op(POOL, lambda e: e.memset(one1_b[:], 1.0), writes=["one1_b"])
        P.op(POOL, lambda e: e.memset(farm[0:64, :], MASKV), writes=["farm"])
        P.op(POOL, lambda e: e.memset(farm[64:128, :], 0.0), writes=["farm"])
        for l in range(2):
            P.dma(POOL, lambda e, l=l: [e.dma_start(out=bnear[:], in_=bnear_d[:, l * NH * 256:(l + 1) * NH * 256])], "c2",
                  writes=["bnear"])
            for h in range(NH):
                P.op(DVE, lambda e, l=l, h=h: e.tensor_scalar(out=bnear[:, h * 256:(h + 1) * 256], in0=bnear[:, h * 256:(h + 1) * 256],
                                                              scalar1=vecs[:, 2 * VPL + 8 + l * NH + h:2 * VPL + 8 + l * NH + h + 1],
                                                              scalar2=None, op0=ALU.subtract),
                     reads=["bnear", "vecs"], writes=["bnear"])
            P.dma(POOL, lambda e, l=l: [e.dma_start(out=bns[l], in_=bnear[:])], "c3", reads=["bnear"], writes=["bns%d" % l])
        CONSTS = ["vecs", "ident_f", "ident_b", "ones_b", "ones_f", "one1_b", "farm", "bnear"]

        P0_ORDER = ["w_in", "w_conv_out", "w_att_out", "w_out", "w_ffn_up", "w_ffn_down", "w_ple_gate", "w_ple_proj"]
        WDIM = {n: (K, N) for (n, K, N) in WSPEC}
        p0_done = set()

        def p0_gen():
            cnt = 0
            for l in range(2):
                for n in P0_ORDER:
                    K, N = WDIM[n]
                    for c0 in range(0, N, 2048):
                        cc = c0 // 2048
                        for rb in range(K // 128):
                            cw = min(2048, N - c0)
                            nb_ = cw // 512
                            i = cnt % 2
                            stg = Vr[i][:, 1, :, :].rearrange("p a c -> p (a c)").bitcast(F32)
                            ob = kT[i][:, 0:nb_, 1, :]
                            sres = ["V%d_1_%d_%d" % (i, tb, nh) for tb in range(4) for nh in range(2)]
                            ores = ["kT%d_1_%d" % (i, bb) for bb in range(4)]
                            src = w_d[n][l, rb * 128:(rb + 1) * 128, c0:c0 + cw]
                            dst = ws[n][l, rb * 128:(rb + 1) * 128, c0:c0 + cw].rearrange("p (a c) -> p a c", c=512)
                            P.dma(SP, lambda e, o=stg[:, 0:cw], s=src: [e.dma_start(out=o, in_=s)], "p0l%d" % i, writes=sres)
                            eng = [ACT, DVE][cnt % 2]
                            copy_op(eng, ob, stg[:, 0:cw].rearrange("p (a c) -> p a c", c=512), sres, ores)
                            P.dma(ACT, lambda e, o=dst, s=ob: [e.dma_start(out=o, in_=s)], "p0s%d" % i,
                                  reads=ores, writes=["ws_%s_%d_%d_%d" % (n, l, cc, i)])
                            cnt += 1
                            yield
                        p0_done.add((n, l, cc))

        p0 = p0_gen()

        def p0_advance(k):
            for _ in range(k):
                try:
                    next(p0)
                except StopIteration:
                    return

        def p0_ensure(n, l, ccs):
            for cc in ccs:
                while (n, l, cc) not in p0_done:
                    try:
                        next(p0)
                    except StopIteration:
                        return

        def wres(n, l, ccs):
            return ["ws_%s_%d_%d_%d" % (n, l, cc, i) for cc in ccs for i in range(2)]

        slot_i = [0]

        def load_chunk(n, l, pieces):
            ccs = sorted(set(c // 2048 for (c0_, cw_) in pieces for c in (c0_, c0_ + cw_ - 1)))
            p0_ensure(n, l, ccs)
            K = [k for (nn, k, _) in WSPEC if nn == n][0]
            KB = K // 128
            W = sum(c for _, c in pieces)
            j = slot_i[0] % NSLOT
            slot_i[0] += 1
            sl = slots[j][:, 0:KB * W].rearrange("p (k w) -> p k w", w=W)
            srcv = ws[n][l].rearrange("(k p) n -> p k n", p=128)

            def fn(e, sl=sl, srcv=srcv, pieces=pieces):
                r = []
                o = 0
                for (c0, cw) in pieces:
                    r.append(e.dma_start(out=sl[:, :, o:o + cw], in_=srcv[:, :, c0:c0 + cw]))
                    o += cw
                return r
            P.dma(SP, fn, "wl%d" % j, reads=wres(n, l, ccs), writes=["slot%d" % j], n=len(pieces))
            p0_advance(2)
            return sl, "slot%d" % j, KB

        def mm_group(bank, bres, Tn, lhs_list, rhs_list, reads):
            def fn(e):
                r = None
                nk = len(lhs_list)
                for k in range(nk):
                    r = e.matmul(bank, lhs_list[k], rhs_list[k], start=(k == 0), stop=(k == nk - 1))
                return r
            P.op(PE, fn, reads=reads, writes=[bres])

        def rms_stats(Tn, src_fn):
            bk, bres = next_bank()
            for b in range(NB):
                s_ap, s_res = src_fn(b)
                j = b % 2
                P.op(ACT, lambda e, o=sq[j][:, 0:Tn], i_=s_ap: e.activation(out=o, in_=i_, func=AF.Square),
                     reads=s_res, writes=["sq%d" % j])
                P.op(PE, lambda e, b=b, j=j: e.matmul(bk[:, 0:Tn], ones_b[:], sq[j][:, 0:Tn], start=(b == 0), stop=(b == NB - 1)),
                     reads=["sq%d" % j, "ones_b"], writes=[bres])
            P.op(ACT, lambda e: e.activation(out=st_tmp[:, 0:Tn], in_=bk[:, 0:Tn], func=AF.Sqrt, bias=EPS, scale=1.0),
                 reads=[bres], writes=["st_tmp"])
            P.op(DVE, lambda e: e.reciprocal(out=st_rstd[:, 0:Tn], in_=st_tmp[:, 0:Tn]), reads=["st_tmp"], writes=["st_rstd"])

        def rmsnorm_to_hT(Tn, gname, l):
            rms_stats(Tn, lambda b: (xT[:, b, 0:Tn], ["xT%d" % b]))
            for b in range(NB):
                P.op(DVE, lambda e, b=b: e.scalar_tensor_tensor(out=hT[:, b, 0:Tn], in0=xT[:, b, 0:Tn], scalar=vcol(gname, l, b),
                                                                in1=st_rstd[:, 0:Tn], op0=ALU.mult, op1=ALU.mult),
                     reads=["xT%d" % b, "st_rstd", "vecs"], writes=["hT%d" % b])

        def hT_src(Tn):
            return [hT[:, kb, 0:Tn] for kb in range(NB)], ["hT%d" % kb for kb in range(NB)]

        def tm_out(bk, bres, tb, cs, outp, outs, l, eng=None):
            stg, sres = next_xs()
            copy_op(eng or evac_eng(), stg[:, 0:512], bk[:, 0:512], [bres], [sres])
            if not cur["last"]:
                r0 = tb * 128 - 64
                if r0 < 0:
                    P.dma(SP, lambda e: [e.dma_start(out=outp[l, 0:64, cs], in_=stg[64:128, 0:512])],
                          "o_" + sres, reads=[sres], is_output=True)
                else:
                    P.dma(SP, lambda e: [e.dma_start(out=outp[l, r0:r0 + 128, cs], in_=stg[:, 0:512])],
                          "o_" + sres, reads=[sres], is_output=True)
            else:
                P.dma(SP, lambda e: [e.dma_start(out=outp[l, 448:512, cs], in_=stg[0:64, 0:512]),
                                       e.dma_start(out=outs[l, 448:512, cs], in_=stg[64:128, 0:512])],
                      "o_" + sres, reads=[sres], n=2, is_output=True)

        cur = {"last": False}
        try:
          for ti, (t0, Tn) in enumerate(TILES):
              last = (ti == 5)
              cur["last"] = last
              half = ti % 2
              ntb = Tn // 128

              def seg(ap, n, stride):
                  if not last:
                      return ap[:, 0:Tn]
                  return ap[:, 0:2 * stride].rearrange("p (s c) -> p s c", c=stride)[:, :, 0:64]

              def pseg(ap):
                  if not last:
                      return ap[:, 0:Tn]
                  return ap[:, 0:128].rearrange("p (s c) -> p s c", c=64)

              for tb in range(ntb):
                  stg, sres = next_xs()
                  P.dma(POOL, lambda e, o=stg[:], s=x_d[t0 + tb * 128:t0 + (tb + 1) * 128, :]: [e.dma_start(out=o, in_=s)],
                        sres, writes=[sres])
                  for hh in range(2):
                      bk, bres = next_bank()

                      def fn(e, stg=stg, bk=bk, hh=hh):
                          r = None
                          for f in range(4):
                              fb = hh * 4 + f
                              r = e.transpose(bk[:, f * 128:(f + 1) * 128], stg[:, fb * 128:(fb + 1) * 128], ident_f[:])
                          return r
                      P.op(PE, fn, reads=[sres, "ident_f"], writes=[bres])
                      copy_op(evac_eng(), xT[:, hh * 4:hh * 4 + 4, tb * 128:(tb + 1) * 128],
                              bk[:, 0:512].rearrange("p (f c) -> p f c", c=128), [bres], ["xT%d" % (hh * 4 + f) for f in range(4)])

              for l in range(2):
                  for tb in range(ntb):
                      stg, sres = next_xs()
                      P.dma(POOL, lambda e, o=stg[:, 0:256], s=p_d[l, t0 + tb * 128:t0 + (tb + 1) * 128, :]: [e.dma_start(out=o, in_=s)],
                            sres, writes=[sres])
                      bk, bres = next_bank()

                      def fn(e, stg=stg, bk=bk):
                          e.transpose(bk[:, 0:128], stg[:, 0:128], ident_f[:])
                          return e.transpose(bk[:, 128:256], stg[:, 128:256], ident_f[:])
                      P.op(PE, fn, reads=[sres, "ident_f"], writes=[bres])
                      copy_op(evac_eng(), pT[:, 0:2, tb * 128:(tb + 1) * 128], bk[:, 0:256].rearrange("p (f c) -> p f c", c=128),
                              [bres], ["pT"])

                  rmsnorm_to_hT(Tn, "norm_mix", l)
                  hsrc, hres = hT_src(Tn)

                  if ti == 0:
                      P.op(POOL, lambda e: e.memset(uT[:, :, 0:30], 0.0), writes=["uT%d" % b for b in range(NB)])
                  else:
                      P.op(POOL, lambda e, l=l: e.tensor_copy(out=uT[:, :, 0:30], in_=uhist[l][:]),
                           reads=["uhist%d" % l], writes=["uT%d" % b for b in range(NB)])
                  if last:
                      stg, sres = next_xs()
                      P.dma(POOL, lambda e, o=stg[0:30, :], s=sconv_d[l]: [e.dma_start(out=o, in_=s)], sres, writes=[sres])
                      bk, bres = next_bank()

                      def fn(e, stg=stg, bk=bk):
                          r = None
                          for fb in range(NB):
                              r = e.transpose(bk[:, fb * 30:(fb + 1) * 30], stg[0:30, fb * 128:(fb + 1) * 128], ident_f[0:30, 0:30])
                          return r
                      P.op(PE, fn, reads=[sres, "ident_f"], writes=[bres])
                      copy_op(ACT, uT[:, :, 94:124], bk[:, 0:240].rearrange("p (f c) -> p f c", c=30), [bres],
                              ["uT%d" % b for b in range(NB)])

                  chk('t%d_l%d_s2' % (ti, l))
                  if last:
                      utm, utm_res = next_xs()
                  for g in range(4):
                      sl, sres_, KB = load_chunk("w_in", l, [(g * 256, 256), (1024 + g * 256, 256)])
                      for j in range(2):
                          blk = 2 * g + j
                          bA, rA = next_bank()
                          mm_group(bA[:, 0:Tn], rA, Tn, [sl[:, kb, j * 128:(j + 1) * 128] for kb in range(KB)], hsrc, hres + [sres_])
                          bB, rB = next_bank()
                          mm_group(bB[:, 0:Tn], rB, Tn, [sl[:, kb, 256 + j * 128:256 + (j + 1) * 128] for kb in range(KB)], hsrc, hres + [sres_])
                          sj = blk % 2
                          P.op(ACT, lambda e, o=sg[sj][:, 0:Tn], i_=bB[:, 0:Tn]: e.activation(out=o, in_=i_, func=AF.Sigmoid),
                               reads=[rB], writes=["sg%d" % sj])
                          P.op(DVE, lambda e, o=seg(uT[:, blk, 30:542], Tn, 94), a=pseg(bA), s_=pseg(sg[sj]): e.tensor_tensor(out=o, in0=a, in1=s_, op=ALU.mult),
                               reads=[rA, "sg%d" % sj], writes=["uT%d" % blk])
                      if last:
                          bT, rT = next_bank()
                          mm_group(bT[:, 0:512], rT, 512, [hT[:, kb, 0:128] for kb in range(KB)], [sl[:, kb, 0:512] for kb in range(KB)],
                                   hres + [sres_])
                          P.op(ACT, lambda e, i_=bT[:, 256:512]: e.activation(out=sg[0][:, 0:256], in_=i_, func=AF.Sigmoid),
                               reads=[rT], writes=["sg0"])
                          P.op(DVE, lambda e, o=utm[:, g * 256:(g + 1) * 256], a=bT[:, 0:256]: e.tensor_tensor(out=o, in0=a, in1=sg[0][:, 0:256], op=ALU.mult),
                               reads=[rT, "sg0"], writes=[utm_res])
                  if last:
                      P.dma(POOL, lambda e, utm=utm, l=l: [e.dma_start(out=convp_o[l], in_=utm[34:64, :]),
                                                           e.dma_start(out=convs_o[l], in_=utm[98:128, :])],
                            "o_" + utm_res, reads=[utm_res], n=2, is_output=True)

                  chk('t%d_l%d_s3' % (ti, l))
                  want_tm = ti >= 4
                  for which in range(2):
                      for g in range(2):
                          sl, sres_, KB = load_chunk("w_in", l, [(2048 + which * 1024 + g * 512, 512)])
                          for j in range(4):
                              blk = 4 * g + j
                              bk, bres = next_bank()
                              mm_group(bk[:, 0:Tn], bres, Tn, [sl[:, kb, j * 128:(j + 1) * 128] for kb in range(KB)], hsrc, hres + [sres_])
                              if which == 0:
                                  qa, qr = qTb(blk)
                                  copy_op(ACT, qa[:, 0:Tn], bk[:, 0:Tn], [bres], qr, scale=0.125)
                              else:
                                  copy_op(DVE, kT[l][:, blk, half, 0:Tn], bk[:, 0:Tn], [bres], ["kT%d_%d_%d" % (l, half, blk)])
                          if which == 1 and want_tm:
                              for tb in range(ntb):
                                  bT, rT = next_bank()
                                  mm_group(bT[:, 0:512], rT, 512, [hT[:, kb, tb * 128:(tb + 1) * 128] for kb in range(KB)],
                                           [sl[:, kb, 0:512] for kb in range(KB)], hres + [sres_])
                                  tm_out(bT, rT, tb, slice(g * 512, (g + 1) * 512), kp_o, ks_o, l)
                  chk('t%d_l%d_s3v' % (ti, l))
                  for nh in range(2):
                      sl, sres_, KB = load_chunk("w_in", l, [(4096 + nh * 512, 512)])
                      for tb in range(ntb):
                          bk, bres = next_bank()
                          mm_group(bk[:, 0:512], bres, 512, [hT[:, kb, tb * 128:(tb + 1) * 128] for kb in range(KB)],
                                   [sl[:, kb, 0:512] for kb in range(KB)], hres + [sres_])
                          veng = evac_eng()
                          copy_op(veng, Vr[l][:, half, tb, nh * 512:(nh + 1) * 512], bk[:, 0:512], [bres],
                                  ["V%d_%d_%d_%d" % (l, half, tb, nh)])
                          if want_tm:
                              tm_out(bk, bres, tb, slice(nh * 512, (nh + 1) * 512), vp_o, vs_o, l, eng=veng)

                  chk('t%d_l%d_s4' % (ti, l))
                  for b in range(NB):
                      ca, cr = cFb(b)
                      bk, bres = next_bank()
                      for g0 in range(0, 31, 4):
                          taps = list(range(g0, min(31, g0 + 4)))
                          rs = []
                          for j in taps:
                              r = dg_i[0] % 8
                              dg_i[0] += 1
                              rs.append(r)
                              P.op(DVE, lambda e, r=r, sc=vcol("conv_dw", l, j * 8 + b): e.tensor_scalar(out=dgr[r][:], in0=ident_b[:], scalar1=sc, scalar2=None, op0=ALU.mult),
                                   reads=["ident_b", "vecs"], writes=["dg%d" % r])

                          def fn(e, taps=taps, rs=rs, bk=bk, b=b, last=last, Tn=Tn):
                              r_ = None
                              for j, r in zip(taps, rs):
                                  if not last:
                                      r_ = e.matmul(bk[:, 0:Tn], dgr[r][:], uT[:, b, j:j + Tn], start=(j == 0), stop=(j == 30))
                                  else:
                                      r_ = e.matmul(bk[:, 0:64], dgr[r][:], uT[:, b, j:j + 64], start=(j == 0), stop=False, skip_group_check=True)
                                      r_ = e.matmul(bk[:, 64:128], dgr[r][:], uT[:, b, 94 + j:94 + j + 64], start=False, stop=(j == 30), skip_group_check=True)
                              return r_
                          P.op(PE, fn, reads=["uT%d" % b] + ["dg%d" % r for r in rs], writes=[bres])
                      P.op(ACT, lambda e, ca=ca, bk=bk, bia=vcol("conv_dw_b", l, b): e.activation(out=ca[:, 0:Tn], in_=bk[:, 0:Tn], func=AF.Identity, bias=bia, scale=1.0),
                           reads=[bres, "vecs"], writes=cr)
                  if not last:
                      P.op(POOL, lambda e, l=l: e.tensor_copy(out=uhist[l][:], in_=uT[:, :, 512:542]),
                           reads=["uT%d" % b for b in range(NB)], writes=["uhist%d" % l])
                  bmu, rmu = next_bank()
                  bms, rms_ = next_bank()
                  for b in range(NB):
                      ca, cr = cFb(b)
                      P.op(PE, lambda e, b=b, ca=ca: e.matmul(bmu[:, 0:Tn], ones_f[:], ca[:, 0:Tn], start=(b == 0), stop=(b == NB - 1)),
                           reads=cr + ["ones_f"], writes=[rmu])
                      j = b % 2
                      P.op(ACT, lambda e, o=sq[j][:, 0:Tn], i_=ca[:, 0:Tn]: e.activation(out=o, in_=i_, func=AF.Square),
                           reads=cr, writes=["sq%d" % j])
                      P.op(PE, lambda e, b=b, j=j: e.matmul(bms[:, 0:Tn], ones_b[:], sq[j][:, 0:Tn], start=(b == 0), stop=(b == NB - 1)),
                           reads=["sq%d" % j, "ones_b"], writes=[rms_])
                  copy_op(ACT, st_mu[:, 0:Tn], bmu[:, 0:Tn], [rmu], ["st_mu"])
                  P.op(DVE, lambda e: e.tensor_tensor(out=st_tmp[:, 0:Tn], in0=st_mu[:, 0:Tn], in1=st_mu[:, 0:Tn], op=ALU.mult),
                       reads=["st_mu"], writes=["st_tmp"])
                  P.op(DVE, lambda e: e.tensor_tensor(out=st_tmp[:, 0:Tn], in0=bms[:, 0:Tn], in1=st_tmp[:, 0:Tn], op=ALU.subtract),
                       reads=[rms_, "st_tmp"], writes=["st_tmp"])
                  P.op(ACT, lambda e: e.activation(out=st_tmp[:, 0:Tn], in_=st_tmp[:, 0:Tn], func=AF.Sqrt, bias=EPS, scale=1.0),
                       reads=["st_tmp"], writes=["st_tmp"])
                  P.op(DVE, lambda e: e.reciprocal(out=st_rstd[:, 0:Tn], in_=st_tmp[:, 0:Tn]), reads=["st_tmp"], writes=["st_rstd"])
                  for b in range(NB):
                      ca, cr = cFb(b)
                      sa, sr = sTb(b)
                      P.op(DVE, lambda e, ca=ca: e.tensor_tensor(out=ca[:, 0:Tn], in0=ca[:, 0:Tn], in1=st_mu[:, 0:Tn], op=ALU.subtract),
                           reads=cr + ["st_mu"], writes=cr)
                      P.op(DVE, lambda e, ca=ca: e.tensor_tensor(out=ca[:, 0:Tn], in0=ca[:, 0:Tn], in1=st_rstd[:, 0:Tn], op=ALU.mult),
                           reads=cr + ["st_rstd"], writes=cr)
                      P.op(ACT, lambda e, ca=ca, sa=sa, b=b, l=l: e.activation(out=sa[:, 0:Tn], in_=ca[:, 0:Tn], func=AF.Silu,
                                                                              bias=vcol("conv_ln_b", l, b), scale=vcol("conv_ln_g", l, b)),
                           reads=cr + ["vecs"], writes=sr)

                  chk('t%d_l%d_s5' % (ti, l))
                  def gates(which):
                      for g in range(2):
                          sl, sres_, KB = load_chunk("w_in", l, [(5120 + which * 1024 + g * 512, 512)])
                          for j in range(4):
                              blk = 4 * g + j
                              bk, bres = next_bank()
                              mm_group(bk[:, 0:Tn], bres, Tn, [sl[:, kb, j * 128:(j + 1) * 128] for kb in range(KB)], hsrc, hres + [sres_])
                              ga, gr = gTb(blk)
                              P.op(ACT, lambda e, ga=ga, bk=bk, blk=blk: e.activation(out=ga[:, 0:Tn], in_=bk[:, 0:Tn], func=AF.Sigmoid,
                                                                                      bias=vcol("b_gate", l, which * 8 + blk), scale=1.0),
                                   reads=[bres, "vecs"], writes=gr)
                  gates(0)
                  ssrc = [sTb(b)[0][:, 0:Tn] for b in range(NB)]
                  ssres = sum([sTb(b)[1] for b in range(NB)], [])
                  for g in range(2):
                      sl, sres_, KB = load_chunk("w_conv_out", l, [(g * 512, 512)])
                      for j in range(4):
                          blk = 4 * g + j
                          bk, bres = next_bank()
                          mm_group(bk[:, 0:Tn], bres, Tn, [sl[:, kb, j * 128:(j + 1) * 128] for kb in range(KB)], ssrc, ssres + [sres_])
                          ga, gr = gTb(blk)
                          ma, mr = mTb(blk)
                          P.op(DVE, lambda e, ma=ma, bk=bk, ga=ga: e.tensor_tensor(out=ma[:, 0:Tn], in0=bk[:, 0:Tn], in1=ga[:, 0:Tn], op=ALU.mult),
                               reads=[bres] + gr, writes=mr)

                  chk('t%d_l%d_s6' % (ti, l))
                  P.dma(SP, lambda e, l=l: [e.dma_start(out=bnear[:], in_=bns[l])], "bnl", reads=["bns%d" % l], writes=["bnear"])

                  def att_all(qc0, qc1, kblocks, LA=3):
                      units = [(i, kb_, s) for i in range(8) for kb_ in kblocks for s in range(2)]
                      nper = 2 * len(kblocks)
                      info = {}

                      def emit_score(u):
                          i, kb_, s = units[u]
                          r0, r1 = kb_["rows"]
                          c0, c1 = kb_["cols"]
                          qa, qr = qTb(i)
                          h = 2 * i + s
                          j = u % 4
                          j2 = u % NPT
                          Sb, Sres = banks[j], "bk%d" % j
                          kap = kb_["kTf"](s, i)
                          qap = qa[64 * s:64 * s + 64, c0:c1]
                          tb_ = bnear[:, h * 256:(h + 1) * 256]

                          def fs(e, Sb=Sb, kap=kap, qap=qap, kb_=kb_, tb_=tb_, r0=r0, r1=r1, c0=c0, c1=c1):
                              r = e.matmul(Sb[r0:r1, c0:c1], kap, qap, start=True, stop=False, skip_group_check=True)
                              for (tc0, ncol, oc0) in kb_["near"]:
                                  r = e.matmul(Sb[r0:r1, oc0:oc0 + ncol], ident_b[r0:r1, r0:r1], tb_[r0:r1, tc0:tc0 + ncol],
                                               start=False, stop=False, skip_group_check=True)
                              if kb_["far"] is not None:
                                  oc0 = kb_["far"]
                                  r = e.matmul(Sb[r0:r1, oc0:oc0 + 64], ident_b[r0:r1, r0:r1], farm[r0:r1, :],
                                               start=False, stop=False, skip_group_check=True)
                              return r
                          P.op(PE, fs, reads=kb_["kres"](i) + qr + ["ident_b", "bnear", "farm"], writes=[Sres])
                          cf = vecs[r0:r1, 2 * VPL + 8 + l * NH + h:2 * VPL + 8 + l * NH + h + 1]
                          P.op(ACT, lambda e, o=PT[j2][r0:r1, c0:c1], i_=Sb[r0:r1, c0:c1], cf=cf: e.activation(out=o, in_=i_, func=AF.Exp, bias=cf, scale=1.0),
                               reads=[Sres, "vecs"], writes=["PT%d" % j2])

                      def emit_pv(u):
                          i, kb_, s = units[u]
                          r0, r1 = kb_["rows"]
                          c0, c1 = kb_["cols"]
                          h = 2 * i + s
                          j2 = u % NPT
                          Ob, Ores = banks[4 + i % 2], "bk%d" % (4 + i % 2)
                          Db, Dres = banks[6 + i % 2], "bk%d" % (6 + i % 2)
                          fst = (u % nper) < 2
                          vap = kb_["V"][r0:r1, h * 64:(h + 1) * 64]

                          def fpv(e, vap=vap, j=j2, r0=r0, r1=r1, c0=c0, c1=c1, s=s, fst=fst, Ob=Ob, Db=Db):
                              e.matmul(Ob[64 * s:64 * s + 64, c0:c1], vap, PT[j][r0:r1, c0:c1], start=fst, stop=False, skip_group_check=True)
                              return e.matmul(Db[64 * s:64 * s + 64, c0:c1], one1_b[r0:r1, :], PT[j][r0:r1, c0:c1], start=fst, stop=False,
                                              skip_group_check=True)
                          P.op(PE, fpv, reads=["PT%d" % j2, "one1_b"] + kb_["vres"], writes=[Ores, Dres])
                          if (u % nper) == nper - 1:
                              aa, ar = aTb(i)
                              P.op(DVE, lambda e, Db=Db: e.reciprocal(out=st_tmp[:, qc0:qc1], in_=Db[:, qc0:qc1]), reads=[Dres], writes=["st_tmp"])
                              P.op(DVE, lambda e, aa=aa, Ob=Ob: e.tensor_tensor(out=aa[:, qc0:qc1], in0=Ob[:, qc0:qc1], in1=st_tmp[:, qc0:qc1], op=ALU.mult),
                                   reads=[Ores, "st_tmp"], writes=ar)

                      for idx in range(len(units) + LA):
                          if idx - LA >= 0:
                              emit_pv(idx - LA)
                          if idx < len(units):
                              emit_score(idx)

                  def kblock_pair(hf, pb, rows, cols, near, far):
                      return dict(rows=rows, cols=cols, near=near, far=far, kT=None,
                                  kTf=lambda s, i, hf=hf, pb=pb, rows=rows: kT[l][64 * s:64 * s + 64, i, hf, pb * 128 + rows[0]:pb * 128 + rows[1]],
                                  kres=lambda i, hf=hf: ["kT%d_%d_%d" % (l, hf, i)],
                                  V=Vr[l][:, hf, pb, :], vres=["V%d_%d_%d_%d" % (l, hf, pb, nh) for nh in range(2)])

                  if not last:
                      kbl = []
                      for b in range(8):
                          if b < 4 and ti == 0:
                              continue
                          ilo = max(0, 2 * b - 8)
                          ihi = min(7, 2 * b + 1)
                          near = []
                          cs_ = [c for c in range(4) if 0 <= 2 * b - 8 + c <= 7]
                          if b >= 3 and cs_:
                              near.append((cs_[0] * 64, len(cs_) * 64, (2 * b - 8 + cs_[0]) * 64))
                          far = (2 * b + 1) * 64 if b <= 3 else None
                          hf = (1 - half) if b < 4 else half
                          kbl.append(kblock_pair(hf, b % 4, (0, 128), (ilo * 64, (ihi + 1) * 64), near, far))
                      kbl.sort(key=lambda d: -(d["cols"][1] - d["cols"][0]))
                      att_all(0, Tn, kbl)
                  else:
                      kbl = []
                      for b in range(4):
                          near = [(128, 64, 0)] if b == 3 else []
                          kbl.append(kblock_pair(1 - half, b, (0, 128), (0, 64), near, None))
                      kbl.append(kblock_pair(half, 0, (0, 128), (0, 64), [(0, 64, 0)], None))
                      att_all(0, 64, kbl)
                      chk('t%d_l%d_s6b' % (ti, l))
                      oh = 1 - half
                      for tb in range(4):
                          stg, sres = next_xs()
                          P.dma(POOL, lambda e, o=stg[:], s=ck_d[l, tb * 128:(tb + 1) * 128, :]: [e.dma_start(out=o, in_=s)], sres, writes=[sres])
                          for hh in range(2):
                              bk, bres = next_bank()

                              def fn(e, stg=stg, bk=bk, hh=hh):
                                  r = None
                                  for f in range(4):
                                      fb = hh * 4 + f
                                      r = e.transpose(bk[:, f * 128:(f + 1) * 128], stg[:, fb * 128:(fb + 1) * 128], ident_f[:])
                                  return r
                              P.op(PE, fn, reads=[sres, "ident_f"], writes=[bres])
                              copy_op(evac_eng(), kT[l][:, hh * 4:hh * 4 + 4, oh, tb * 128:(tb + 1) * 128],
                                      bk[:, 0:512].rearrange("p (f c) -> p f c", c=128), [bres],
                                      ["kT%d_%d_%d" % (l, oh, hh * 4 + f) for f in range(4)])
                      P.dma(POOL, lambda e, l=l, oh=oh: [e.dma_start(out=Vr[l][:, oh, :, :], in_=cv_d[l].rearrange("(b p) n -> p b n", p=128))],
                            "cvl", writes=["V%d_%d_%d_%d" % (l, oh, tb, nh) for tb in range(4) for nh in range(2)])
                      chk('t%d_l%d_s6c' % (ti, l))
                      P.dma(POOL, lambda e, l=l: [e.dma_start(out=ks_o[l, 0:448, :], in_=ck_d[l, 64:512, :]),
                                                 e.dma_start(out=vs_o[l, 0:448, :], in_=cv_d[l, 64:512, :])], "occ", n=2, is_output=True)
                      chk('t%d_l%d_s6d' % (ti, l))
                      kbl = []
                      for b in range(4):
                          near = [(128, 64, 64)] if b == 3 else []
                          kbl.append(kblock_pair(oh, b, (0, 128), (64, 128), near, None))
                      kbl.append(kblock_pair(half, 0, (0, 128), (64, 128), [(64, 64, 64)], 64))
                      att_all(64, 128, kbl)

                  chk('t%d_l%d_s7' % (ti, l))
                  gates(1)
                  asrc = [aTb(b)[0][:, 0:Tn] for b in range(NB)]
                  asres = sum([aTb(b)[1] for b in range(NB)], [])
                  for g in range(2):
                      sl, sres_, KB = load_chunk("w_att_out", l, [(g * 512, 512)])
                      for j in range(4):
                          blk = 4 * g + j
                          bk, bres = next_bank()
                          mm_group(bk[:, 0:Tn], bres, Tn, [sl[:, kb, j * 128:(j + 1) * 128] for kb in range(KB)], asrc, asres + [sres_])
                          ga, gr = gTb(blk)
                          ma, mr = mTb(blk)
                          tj = blk % 2
                          P.op(DVE, lambda e, bk=bk, ga=ga, tj=tj: e.tensor_tensor(out=tmpb[tj][:, 0:Tn], in0=bk[:, 0:Tn], in1=ga[:, 0:Tn], op=ALU.mult),
                               reads=[bres] + gr, writes=["tmpb%d" % tj])
                          P.op(POOL, lambda e, ma=ma, tj=tj: e.tensor_tensor(out=ma[:, 0:Tn], in0=ma[:, 0:Tn], in1=tmpb[tj][:, 0:Tn], op=ALU.add),
                               reads=["tmpb%d" % tj] + mr, writes=mr)

                  msrc = [mTb(b)[0][:, 0:Tn] for b in range(NB)]
                  msres = sum([mTb(b)[1] for b in range(NB)], [])
                  for g in range(2):
                      sl, sres_, KB = load_chunk("w_out", l, [(g * 512, 512)])
                      for j in range(4):
                          blk = 4 * g + j
                          bk, bres = next_bank()
                          mm_group(bk[:, 0:Tn], bres, Tn, [sl[:, kb, j * 128:(j + 1) * 128] for kb in range(KB)], msrc, msres + [sres_])
                          P.op(DVE, lambda e, bk=bk, blk=blk: e.tensor_tensor(out=xT[:, blk, 0:Tn], in0=bk[:, 0:Tn], in1=xT[:, blk, 0:Tn], op=ALU.add),
                               reads=[bres, "xT%d" % blk], writes=["xT%d" % blk])

                  chk('t%d_l%d_s9' % (ti, l))
                  rmsnorm_to_hT(Tn, "norm_ffn", l)
                  hsrc, hres = hT_src(Tn)
                  if last:
                      bkS, bresS = next_bank()
                      for pi, (c0, cw) in enumerate([(0, 1024), (1024, 1024), (2048, 768)]):
                          stg, sres = next_xs()
                          P.dma(POOL, lambda e, o=stg[0:2, 0:cw], s=sffn_d[l, :, c0:c0 + cw]: [e.dma_start(out=o, in_=s)], sres, writes=[sres])

                          def fn(e, stg=stg, bkS=bkS, pi=pi, cw=cw):
                              r = None
                              for q in range(cw // 128):
                                  fb = pi * 8 + q
                                  r = e.transpose(bkS[:, fb * 2:(fb + 1) * 2], stg[0:2, q * 128:(q + 1) * 128], ident_f[0:2, 0:2])
                              return r
                          P.op(PE, fn, reads=[sres, "ident_f"], writes=[bresS])
                      copy_op(ACT, sfT[:], bkS[:, 0:2 * FB].rearrange("p (f c) -> p f c", c=2), [bresS], ["sfT"])
                  for g in range(11):
                      sl, sres_, KB = load_chunk("w_ffn_up", l, [(g * 256, 256), (DFF + g * 256, 256)])
                      for j in range(2):
                          fb = 2 * g + j
                          fj = fb % 2
                          ur = ["upr%d" % fj]
                          ua = upr[fj]
                          if ti == 0:
                              P.op(POOL, lambda e, ua=ua: e.memset(ua[:, 0:2], 0.0), writes=ur)
                          else:
                              P.op(POOL, lambda e, ua=ua, fb=fb, l=l: e.tensor_copy(out=ua[:, 0:2], in_=uphist[l][:, fb, :]),
                                   reads=["uph%d_%d" % (l, fb)], writes=ur)
                          if last:
                              P.op(POOL, lambda e, ua=ua, fb=fb: e.tensor_copy(out=ua[:, 66:68], in_=sfT[:, fb, :]), reads=["sfT"], writes=ur)
                          bU, rU = next_bank()
                          mm_group(bU[:, 0:Tn], rU, Tn, [sl[:, kb, j * 128:(j + 1) * 128] for kb in range(KB)], hsrc, hres + [sres_])
                          bG, rG = next_bank()
                          mm_group(bG[:, 0:Tn], rG, Tn, [sl[:, kb, 256 + j * 128:256 + (j + 1) * 128] for kb in range(KB)], hsrc, hres + [sres_])
                          fa, fr = fTb(fb)
                          accv = pseg(facc[fj])
                          copy_op(ACT, seg(ua[:, 2:514], Tn, 66), pseg(bU), [rU], ur)
                          P.op(ACT, lambda e, accv=accv, src=pseg(bU), bia=vcol("ffn_dw_b", l, fb), sc=vcol("ffn_dw", l, 2 * FB + fb):
                               e.activation(out=accv, in_=src, func=AF.Identity, bias=bia, scale=sc),
                               reads=[rU, "vecs"], writes=["facc%d" % fj])
                          for tap in (1, 0):
                              P.op(DVE, lambda e, accv=accv, src=seg(ua[:, tap:514], Tn, 66), sc=vcol("ffn_dw", l, tap * FB + fb): e.scalar_tensor_tensor(
                                  out=accv, in0=src, scalar=sc, in1=accv, op0=ALU.mult, op1=ALU.add),
                                  reads=ur + ["vecs", "facc%d" % fj], writes=["facc%d" % fj])
                          if not last:
                              P.op(POOL, lambda e, ua=ua, fb=fb, l=l: e.tensor_copy(out=uphist[l][:, fb, :], in_=ua[:, 512:514]),
                                   reads=ur, writes=["uph%d_%d" % (l, fb)])
                          P.op(ACT, lambda e, o=sg[fj][:, 0:Tn], i_=facc[fj][:, 0:Tn]: e.activation(out=o, in_=i_, func=AF.Gelu),
                               reads=["facc%d" % fj], writes=["sg%d" % fj])
                          P.op(DVE, lambda e, o=fa[:, 0:Tn], a=bG[:, 0:Tn], b_=sg[fj][:, 0:Tn]: e.tensor_tensor(out=o, in0=a, in1=b_, op=ALU.mult),
                               reads=[rG, "sg%d" % fj], writes=fr)
                      if last:
                          bT, rT = next_bank()
                          mm_group(bT[:, 0:256], rT, 256, [hT[:, kb, 0:128] for kb in range(KB)], [sl[:, kb, 0:256] for kb in range(KB)],
                                   hres + [sres_])
                          sm = g % 2
                          copy_op(evac_eng(), smst[:, sm, :], bT[:, 0:256], [rT], ["smst%d" % sm])
                          P.dma(POOL, lambda e, g=g, l=l, sm=sm: [e.dma_start(out=ffnp_o[l, :, g * 256:(g + 1) * 256], in_=smst[62:64, sm, :]),
                                                                 e.dma_start(out=ffns_o[l, :, g * 256:(g + 1) * 256], in_=smst[126:128, sm, :])],
                                "o_smst%d" % sm, reads=["smst%d" % sm], n=2, is_output=True)

                  chk('t%d_l%d_s11' % (ti, l))
                  fsrc = [fTb(fb)[0][:, 0:Tn] for fb in range(FB)]
                  fsres = sorted(set(sum([fTb(fb)[1] for fb in range(FB)], [])))
                  for blk in range(8):
                      sl, sres_, KB = load_chunk("w_ffn_down", l, [(blk * 128, 128)])
                      bk, bres = next_bank()
                      mm_group(bk[:, 0:Tn], bres, Tn, [sl[:, kb, 0:128] for kb in range(KB)], fsrc, fsres + [sres_])
                      P.op(DVE, lambda e, bk=bk, blk=blk: e.tensor_tensor(out=xT[:, blk, 0:Tn], in0=bk[:, 0:Tn], in1=xT[:, blk, 0:Tn], op=ALU.add),
                           reads=[bres, "xT%d" % blk], writes=["xT%d" % blk])

                  chk('t%d_l%d_s12' % (ti, l))
                  rmsnorm_to_hT(Tn, "norm_ple", l)
                  hsrc, hres = hT_src(Tn)
                  p0_ensure("w_ple_proj", l, [0])
                  P.dma(SP, lambda e, l=l: [e.dma_start(out=pslot[:], in_=ws["w_ple_proj"][l].rearrange("(k p) n -> p k n", p=128))],
                        "wlp", reads=wres("w_ple_proj", l, [0]), writes=["pslot"])
                  for g in range(2):
                      sl, sres_, KB = load_chunk("w_ple_gate", l, [(g * 512, 512)])
                      for j in range(4):
                          blk = 4 * g + j
                          bG, rG = next_bank()
                          mm_group(bG[:, 0:Tn], rG, Tn, [sl[:, kb, j * 128:(j + 1) * 128] for kb in range(KB)], hsrc, hres + [sres_])
                          bP, rP = next_bank()
                          mm_group(bP[:, 0:Tn], rP, Tn, [pslot[:, kb, blk * 128:(blk + 1) * 128] for kb in range(2)],
                                   [pT[:, kb, 0:Tn] for kb in range(2)], ["pslot", "pT"])
                          sj = blk % 2
                          P.op(ACT, lambda e, sj=sj, bG=bG: e.activation(out=sg[sj][:, 0:Tn], in_=bG[:, 0:Tn], func=AF.Sigmoid),
                               reads=[rG], writes=["sg%d" % sj])
                          P.op(DVE, lambda e, sj=sj, bP=bP: e.tensor_tensor(out=facc[sj][:, 0:Tn], in0=bP[:, 0:Tn], in1=sg[sj][:, 0:Tn], op=ALU.mult),
                               reads=[rP, "sg%d" % sj], writes=["facc%d" % sj])
                          P.op(DVE, lambda e, sj=sj, blk=blk: e.tensor_tensor(out=xT[:, blk, 0:Tn], in0=facc[sj][:, 0:Tn], in1=xT[:, blk, 0:Tn], op=ALU.add),
                               reads=["facc%d" % sj, "xT%d" % blk], writes=["xT%d" % blk])

              chk('t%d_fin' % ti)
              p0_advance(100000)
              rms_stats(Tn, lambda b: (xT[:, b, 0:Tn], ["xT%d" % b]))
              for b in range(NB):
                  ca, cr = cFb(b)
                  P.op(DVE, lambda e, b=b, ca=ca: e.scalar_tensor_tensor(out=ca[:, 0:Tn], in0=xT[:, b, 0:Tn], scalar=vecs[:, 2 * VPL + b:2 * VPL + b + 1],
                                                                       in1=st_rstd[:, 0:Tn], op0=ALU.mult, op1=ALU.mult),
                       reads=["xT%d" % b, "st_rstd", "vecs"], writes=cr)
              for tb in range(ntb):
                  stg, sres = next_xs()
                  for hh in range(2):
                      bk, bres = next_bank()

                      def fn(e, bk=bk, hh=hh, tb=tb):
                          r = None
                          for f in range(4):
                              ca, _ = cFb(hh * 4 + f)
                              r = e.transpose(bk[:, f * 128:(f + 1) * 128], ca[:, tb * 128:(tb + 1) * 128], ident_f[:])
                          return r
                      P.op(PE, fn, reads=sum([cFb(hh * 4 + f)[1] for f in range(4)], []) + ["ident_f"], writes=[bres])
                      copy_op(evac_eng(), stg[:, hh * 512:(hh + 1) * 512], bk[:, 0:512], [bres], [sres])
                  P.dma(POOL, lambda e, stg=stg, r0=t0 + tb * 128: [e.dma_start(out=y_o[r0:r0 + 128, :], in_=stg[:])],
                        "o_" + sres, reads=[sres], is_output=True)

        except _Stop:
            pass
        P.finish(POOL)
        P.emit()
    return nc


_NC = None


def _lay(v):
    v = np.asarray(v, np.float32)
    lead = v.shape[:-1]
    nb = v.shape[-1] // 128
    v = v.reshape(lead + (nb, 128))
    v = np.moveaxis(v, -1, 0)
    return v.reshape(128, -1)


def kernel(**inp):
    global _NC
    f = lambda k: np.ascontiguousarray(np.asarray(inp[k], np.float32))
    x_prompt, x_sample, p_prompt, p_sample = f("x_prompt"), f("x_sample"), f("p_prompt"), f("p_sample")
    ck, cv, sconv, sffn = f("cache_att_k"), f("cache_att_v"), f("state_conv"), f("state_ffn_conv")
    rel = f("rel_table")

    cols = []
    for l in range(2):
        cols += [_lay(f("norm_mix")[l]), _lay(f("conv_dw")[l]), _lay(f("conv_dw_b")[l]), _lay(f("conv_ln_g")[l]),
                 _lay(f("conv_ln_b")[l]), _lay(f("b_gate")[l]), _lay(f("norm_ffn")[l]), _lay(f("ffn_dw")[l]),
                 _lay(f("ffn_dw_b")[l]), _lay(f("norm_ple")[l])]
    cols.append(_lay(f("norm_final")))
    cfar = np.broadcast_to(rel[:, :, 256].reshape(1, 32), (128, 32))
    cols.append(cfar)
    vecs = np.ascontiguousarray(np.concatenate(cols, axis=1), np.float32)
    assert vecs.shape == (128, NV), vecs.shape

    kl = np.arange(64)[:, None]
    ql = np.arange(64)[None, :]
    bn = np.zeros((2, NH, 128, 256), np.float32)
    for o in range(4):
        idx = np.clip(64 * o + ql - kl, -128, 128) + 128
        Bo = rel[:, :, idx]
        bn[:, :, 0:64, o * 64:(o + 1) * 64] = Bo
        if o < 3:
            bn[:, :, 64:128, (o + 1) * 64:(o + 2) * 64] = Bo
    bn[:, :, 64:128, 0:64] = MASKV
    bn_raw = bn
    bnear = np.ascontiguousarray(np.moveaxis(bn_raw, 2, 0).reshape(128, 2 * NH * 256))
    ident = np.eye(128, dtype=np.float32)

    in_maps = []
    for c in range(8):
        s, hb = c // 2, c % 2
        ch0 = 0 if hb == 0 else 64 - NCHK
        xs_ = np.concatenate([x_prompt[s, ch0 * 64:(ch0 + NCHK) * 64], x_sample[c]], axis=0)
        ps_ = np.concatenate([p_prompt[:, s, ch0 * 64:(ch0 + NCHK) * 64], p_sample[:, c]], axis=1)
        m = {"x": np.ascontiguousarray(xs_), "p": np.ascontiguousarray(ps_),
             "ck": np.ascontiguousarray(ck[:, c].reshape(2, 512, D)), "cv": np.ascontiguousarray(cv[:, c].reshape(2, 512, D)),
             "sconv": np.ascontiguousarray(sconv[:, c]), "sffn": np.ascontiguousarray(sffn[:, c]),
             "vecs": vecs, "bnear": bnear, "ident": ident}
        for (n, K, N) in WSPEC:
            m[n] = f(n)
        in_maps.append(m)

    if _NC is None:
        _NC = build_nc()
    res = run_bass_kernel_spmd(_NC, in_maps, core_ids=list(range(8)))
    R = res.results

    y_prompt = np.zeros((4, 4096, D), np.float32)
    y_sample = np.zeros((8, 64, D), np.float32)
    nkp = np.zeros((2, 4, 512, NH, 64), np.float32)
    nvp = np.zeros_like(nkp)
    ncp = np.zeros((2, 4, 30, D), np.float32)
    nfp = np.zeros((2, 4, 2, DFF), np.float32)
    nks = np.zeros((2, 8, 512, NH, 64), np.float32)
    nvs = np.zeros_like(nks)
    ncs = np.zeros((2, 8, 30, D), np.float32)
    nfs = np.zeros((2, 8, 2, DFF), np.float32)
    for c in range(8):
        s, hb = c // 2, c % 2
        r = R[c]
        if hb == 0:
            y_prompt[s, 0:NCHK * 64] = r["y"][0:NCHK * 64]
        else:
            y_prompt[s, NCHK * 64:] = r["y"][HALO * 64:NCHK * 64]
            nkp[:, s] = r["kp"].reshape(2, 512, NH, 64)
            nvp[:, s] = r["vp"].reshape(2, 512, NH, 64)
            ncp[:, s] = r["convp"]
            nfp[:, s] = r["ffnp"]
        y_sample[c] = r["y"][NCHK * 64:]
        nks[:, c] = r["ks"].reshape(2, 512, NH, 64)
        nvs[:, c] = r["vs"].reshape(2, 512, NH, 64)
        ncs[:, c] = r["convs"]
        nfs[:, c] = r["ffns"]
    return (y_prompt, y_sample, nkp, nvp, ncp, nfp, nks, nvs, ncs, nfs)
```
